# Optimizing a Trainium2 kernel written in Bass

```python
import jax
import jax.numpy as jnp
from jax import lax
import numpy as np

D_MODEL = 2048
BATCH = 4
SEQ = 4096
DEPTH = 4

N_A_LAYERS = DEPTH // 2
N_B_LAYERS = DEPTH - N_A_LAYERS
NORM_EPS = 1e-6

RWKV_HEAD_SIZE = 64
RWKV_HEADS = D_MODEL // RWKV_HEAD_SIZE
DECAY_LORA = 96
AAA_LORA = 96
MV_LORA = 64
GN_EPS = 64e-5

NSA_HEADS = 16
NSA_GROUPS = 4
NSA_REP = NSA_HEADS // NSA_GROUPS
NSA_HEAD_DIM = D_MODEL // NSA_HEADS
NSA_WIDTH = NSA_HEADS * NSA_HEAD_DIM
KV_WIDTH = NSA_GROUPS * NSA_HEAD_DIM
N_BRANCH = 3
CMP_BLOCK = 32
CMP_STRIDE = 16
CMP_HIDDEN = 256
SEL_BLOCK = 64
SEL_TOP_N = 16
WINDOW = 512
Q_BLOCK = 64
ROPE_DIM = NSA_HEAD_DIM // 4
ROPE_THETA = 500000.0
NEG_INF = -1e30
FORCE_BONUS = 1e4

kernel_name = 'hybrid_rwkv7_nsa_yoco'


def rmsnorm(x, g):
    xf = x.astype(jnp.float32)
    y = xf * lax.rsqrt(jnp.mean(xf * xf, axis=-1, keepdims=True) + NORM_EPS)
    return (y * g.astype(jnp.float32)).astype(x.dtype)


def rope_partial(x, pos):
    half = ROPE_DIM // 2
    inv = ROPE_THETA ** (-jnp.arange(half, dtype=jnp.float32) / half)
    ang = pos.astype(jnp.float32)[:, None] * inv[None, :]
    cos = jnp.cos(ang)[None, :, None, :]
    sin = jnp.sin(ang)[None, :, None, :]
    xr = x[..., :ROPE_DIM].astype(jnp.float32)
    x1, x2 = xr[..., :half], xr[..., half:]
    rot = jnp.concatenate([x1 * cos - x2 * sin, x2 * cos + x1 * sin], axis=-1)
    return jnp.concatenate([rot.astype(x.dtype), x[..., ROPE_DIM:]], axis=-1)


def token_shift(x):
    return jnp.pad(x, ((0, 0), (1, 0), (0, 0)))[:, :-1]


def wkv7_scan(r, w, k, v, a, b):
    B, T, H, N = r.shape
    seq = tuple(jnp.moveaxis(t.astype(jnp.float32), 1, 0) for t in (r, w, k, v, a, b))

    def step(S, inp):
        r_t, w_t, k_t, v_t, a_t, b_t = inp
        sa = jnp.einsum('bhvk,bhk->bhv', S, a_t)
        S = S * w_t[:, :, None, :] + sa[..., None] * b_t[:, :, None, :] + v_t[..., None] * k_t[:, :, None, :]
        return S, jnp.einsum('bhvk,bhk->bhv', S, r_t)

    S0 = jnp.zeros((B, H, N, N), jnp.float32)
    _, o = lax.scan(step, S0, seq)
    return jnp.moveaxis(o, 0, 1)


def rwkv7_mixer(u, v_first, mu, w_rkvz, w0, w1, w2, a0, a1, a2, vres, k_k, k_a, r_k, ln_g, ln_b, w_o):
    B, T, D = u.shape
    xx = token_shift(u) - u
    mixed = u[None] + xx[None] * mu[:, None, None, :]
    rkvz = jnp.einsum('nbtd,nde->nbte', mixed[:4], w_rkvz)
    r, k, v, z = rkvz[0], rkvz[1], rkvz[2], rkvz[3]
    xv, xw, xa = mixed[2], mixed[4], mixed[5]
    w_log = -jax.nn.softplus(-(w0 + jnp.tanh(xw @ w1) @ w2)) - 0.5
    decay = jnp.exp(-jnp.exp(w_log.astype(jnp.float32)))
    a = jax.nn.sigmoid(a0 + (xa @ a1) @ a2)
    if vres is not None:
        v0, v1, v2 = vres
        v = v + (v_first - v) * jax.nn.sigmoid(v0 + (xv @ v1) @ v2)
    heads = lambda t: t.reshape(B, T, RWKV_HEADS, RWKV_HEAD_SIZE)
    kk = heads(k * k_k).astype(jnp.float32)
    kk = kk / jnp.maximum(jnp.sqrt(jnp.sum(kk * kk, axis=-1, keepdims=True)), 1e-12)
    k = k * (1.0 + (a - 1.0) * k_a)
    rh, kh, vh, ah = heads(r), heads(k), heads(v), heads(a)
    o = wkv7_scan(rh, heads(decay), kh, vh, -kk, kk * ah.astype(jnp.float32))
    mean = jnp.mean(o, axis=-1, keepdims=True)
    var = jnp.mean(jnp.square(o - mean), axis=-1, keepdims=True)
    o = ((o - mean) * lax.rsqrt(var + GN_EPS)).reshape(B, T, D) * ln_g + ln_b
    bonus = jnp.sum((rh * kh * r_k).astype(jnp.float32), axis=-1, keepdims=True) * vh
    o = (o + bonus.reshape(B, T, D)) * jax.nn.silu(z)
    return o.astype(u.dtype) @ w_o, v


def nsa_shared_kv(h, kv_norm_g, kv_w, cmp_pe, cmp_w1, cmp_w2):
    B, T, D = h.shape
    hn = rmsnorm(h, kv_norm_g)
    kv = (hn @ kv_w).reshape(B, T, 6, NSA_GROUPS, NSA_HEAD_DIM)
    k_c, v_c, k_s, v_s, k_w, v_w = [kv[:, :, i] for i in range(6)]
    pos = jnp.arange(T)
    k_s = rope_partial(k_s, pos)
    k_w = rope_partial(k_w, pos)
    n_cmp = (T - CMP_BLOCK) // CMP_STRIDE + 1
    idx = jnp.arange(n_cmp)[:, None] * CMP_STRIDE + jnp.arange(CMP_BLOCK)[None, :]

    def compress(t, j):
        blocks = t[:, idx] + cmp_pe[j][None, None, :, None, :]
        blocks = jnp.moveaxis(blocks, 3, 2).reshape(B, n_cmp, NSA_GROUPS, CMP_BLOCK * NSA_HEAD_DIM)
        return jax.nn.silu(blocks @ cmp_w1[j]) @ cmp_w2[j]

    k_cmp = rope_partial(compress(k_c, 0), idx[:, -1])
    v_cmp = compress(v_c, 1)
    to_g = lambda t: jnp.moveaxis(t, 2, 1)
    n_blk = T // SEL_BLOCK
    k_sel = to_g(k_s).reshape(B, NSA_GROUPS, n_blk, SEL_BLOCK, NSA_HEAD_DIM)
    v_sel = to_g(v_s).reshape(B, NSA_GROUPS, n_blk, SEL_BLOCK, NSA_HEAD_DIM)
    pad = ((0, 0), (0, 0), (WINDOW, 0), (0, 0))
    k_win = jnp.pad(to_g(k_w), pad)
    v_win = jnp.pad(to_g(v_w), pad)
    return (to_g(k_cmp), to_g(v_cmp), k_sel, v_sel, k_win, v_win)


def nsa_mixer(u, shared, w_in, gate_b, w_o):
    B, T, D = u.shape
    H, G, R, d = NSA_HEADS, NSA_GROUPS, NSA_REP, NSA_HEAD_DIM
    k_cmp, v_cmp, k_sel, v_sel, k_win, v_win = shared
    proj = u @ w_in
    q = rope_partial(proj[..., :NSA_WIDTH].reshape(B, T, H, d), jnp.arange(T))
    gl = proj[..., NSA_WIDTH:NSA_WIDTH + N_BRANCH * H] + gate_b
    z = proj[..., NSA_WIDTH + N_BRANCH * H:]
    gates = jax.nn.sigmoid(gl.astype(jnp.float32)).reshape(B, T, G, R, N_BRANCH)
    nq = T // Q_BLOCK
    q_blocks = q.reshape(B, nq, Q_BLOCK, G, R, d).transpose(1, 0, 3, 4, 2, 5)
    g_blocks = gates.reshape(B, nq, Q_BLOCK, G, R, N_BRANCH).transpose(1, 0, 3, 4, 2, 5)
    n_cmp = k_cmp.shape[2]
    n_blk = k_sel.shape[2]
    n_sel = min(SEL_TOP_N, n_blk)
    cmp_start = jnp.arange(n_cmp) * CMP_STRIDE
    cmp_end = cmp_start + CMP_BLOCK - 1
    cpos = cmp_start[:, None] + jnp.arange(CMP_BLOCK)[None, :]
    overlap = jax.nn.one_hot(cpos // SEL_BLOCK, n_blk, dtype=jnp.float32).mean(axis=1)
    blk = jnp.arange(n_blk)
    scale = d ** -0.5
    b_ix = jnp.arange(B)[:, None, None, None]
    g_ix = jnp.arange(G)[None, :, None, None]

    def block_fn(args):
        qb, gb, start = args
        tq = start + jnp.arange(Q_BLOCK)
        s = jnp.einsum('bgrqd,bgcd->bgrqc', qb, k_cmp).astype(jnp.float32) * scale
        vis = cmp_end[None, :] <= tq[:, None]
        p_cmp = jax.nn.softmax(jnp.where(vis, s, NEG_INF), axis=-1) * jnp.any(vis, axis=-1)[:, None]
        o_cmp = jnp.einsum('bgrqc,bgcd->bgrqd', p_cmp.astype(v_cmp.dtype), v_cmp)
        imp = jnp.einsum('bgrqc,cj->bgqj', p_cmp, overlap)
        cur = tq // SEL_BLOCK
        forced = (blk[None, :] == 0) | (blk[None, :] == cur[:, None]) | (blk[None, :] == cur[:, None] - 1)
        future = blk[None, :] > cur[:, None]
        imp = jnp.where(forced, FORCE_BONUS, jnp.where(future, NEG_INF, imp))
        _, sel = lax.top_k(imp, n_sel)
        ks = k_sel[b_ix, g_ix, sel]
        vs = v_sel[b_ix, g_ix, sel]
        s = jnp.einsum('bgrqd,bgqsld->bgrqsl', qb, ks).astype(jnp.float32) * scale
        kpos = sel[..., None] * SEL_BLOCK + jnp.arange(SEL_BLOCK)
        ok = kpos <= tq[None, None, :, None, None]
        s = jnp.where(ok[:, :, None], s, NEG_INF)
        p = jax.nn.softmax(s.reshape(B, G, R, Q_BLOCK, n_sel * SEL_BLOCK), axis=-1).reshape(s.shape)
        o_sel = jnp.einsum('bgrqsl,bgqsld->bgrqd', p.astype(vs.dtype), vs)
        kw = lax.dynamic_slice_in_dim(k_win, start, WINDOW + Q_BLOCK, axis=2)
        vw = lax.dynamic_slice_in_dim(v_win, start, WINDOW + Q_BLOCK, axis=2)
        wpos = start - WINDOW + jnp.arange(WINDOW + Q_BLOCK)
        diff = tq[:, None] - wpos[None, :]
        okw = (diff >= 0) & (diff < WINDOW) & (wpos[None, :] >= 0)
        s = jnp.einsum('bgrqd,bgkd->bgrqk', qb, kw).astype(jnp.float32) * scale
        p = jax.nn.softmax(jnp.where(okw, s, NEG_INF), axis=-1)
        o_win = jnp.einsum('bgrqk,bgkd->bgrqd', p.astype(vw.dtype), vw)
        o = gb[..., 0:1] * o_cmp + gb[..., 1:2] * o_sel + gb[..., 2:3] * o_win
        return o.astype(qb.dtype)

    starts = jnp.arange(nq) * Q_BLOCK
    o = lax.map(block_fn, (q_blocks, g_blocks, starts))
    o = o.transpose(1, 0, 4, 2, 3, 5).reshape(B, T, NSA_WIDTH)
    o = o * jax.nn.silu(z)
    return o @ w_o


def setup_inputs(seed: int = 0) -> dict:
    key = jax.random.key(seed)
    ks = jax.random.split(key, 32)
    D = D_MODEL
    na, nb = N_A_LAYERS, N_B_LAYERS
    nv = max(na - 1, 0)
    nrm = lambda i, shape, s: jax.random.normal(ks[i], shape, jnp.float32) * s
    return {
        'x': nrm(0, (BATCH, SEQ, D), 1.0),
        'a_norm_g': 1.0 + nrm(1, (na, D), 0.02),
        'a_mu': jax.random.uniform(ks[2], (na, 6, D), jnp.float32),
        'a_w_rkvz': nrm(3, (na, 4, D, D), D ** -0.5),
        'a_w0': jax.random.uniform(ks[4], (na, D), jnp.float32, minval=-6.0, maxval=0.0),
        'a_w1': nrm(5, (na, D, DECAY_LORA), D ** -0.5),
        'a_w2': nrm(6, (na, DECAY_LORA, D), 0.1 * DECAY_LORA ** -0.5),
        'a_a0': nrm(7, (na, D), 0.1),
        'a_a1': nrm(8, (na, D, AAA_LORA), D ** -0.5),
        'a_a2': nrm(9, (na, AAA_LORA, D), 0.1 * AAA_LORA ** -0.5),
        'a_v0': nrm(10, (nv, D), 0.1),
        'a_v1': nrm(11, (nv, D, MV_LORA), D ** -0.5),
        'a_v2': nrm(12, (nv, MV_LORA, D), 0.1 * MV_LORA ** -0.5),
        'a_k_k': 0.85 + nrm(13, (na, D), 0.02),
        'a_k_a': 1.0 + nrm(14, (na, D), 0.02),
        'a_r_k': nrm(15, (na, RWKV_HEADS, RWKV_HEAD_SIZE), 0.1),
        'a_ln_g': 1.0 + nrm(16, (na, D), 0.02),
        'a_ln_b': nrm(17, (na, D), 0.02),
        'a_w_o': nrm(18, (na, D, D), D ** -0.5),
        'kv_norm_g': 1.0 + nrm(19, (D,), 0.02),
        'kv_w': nrm(20, (D, 6 * KV_WIDTH), D ** -0.5),
        'cmp_pe': nrm(21, (2, CMP_BLOCK, NSA_HEAD_DIM), 0.1),
        'cmp_w1': nrm(22, (2, CMP_BLOCK * NSA_HEAD_DIM, CMP_HIDDEN), (CMP_BLOCK * NSA_HEAD_DIM) ** -0.5),
        'cmp_w2': nrm(23, (2, CMP_HIDDEN, NSA_HEAD_DIM), CMP_HIDDEN ** -0.5),
        'b_norm_g': 1.0 + nrm(24, (nb, D), 0.02),
        'b_w_in': nrm(25, (nb, D, 2 * NSA_WIDTH + N_BRANCH * NSA_HEADS), D ** -0.5),
        'b_gate_b': nrm(26, (nb, N_BRANCH * NSA_HEADS), 0.1),
        'b_w_o': nrm(27, (nb, NSA_WIDTH, D), NSA_WIDTH ** -0.5),
        'final_g': 1.0 + nrm(28, (D,), 0.02),
    }


def reference(x, a_norm_g, a_mu, a_w_rkvz, a_w0, a_w1, a_w2, a_a0, a_a1, a_a2, a_v0, a_v1, a_v2,
              a_k_k, a_k_a, a_r_k, a_ln_g, a_ln_b, a_w_o, kv_norm_g, kv_w, cmp_pe, cmp_w1, cmp_w2,
              b_norm_g, b_w_in, b_gate_b, b_w_o, final_g):
    h = x
    v_first = None
    shared = None
    for layer in range(DEPTH):
        if layer < N_A_LAYERS:
            i = layer
            vres = None if i == 0 else (a_v0[i - 1], a_v1[i - 1], a_v2[i - 1])
            y, v = rwkv7_mixer(rmsnorm(h, a_norm_g[i]), v_first, a_mu[i], a_w_rkvz[i],
                               a_w0[i], a_w1[i], a_w2[i], a_a0[i], a_a1[i], a_a2[i], vres,
                               a_k_k[i], a_k_a[i], a_r_k[i], a_ln_g[i], a_ln_b[i], a_w_o[i])
            if i == 0:
                v_first = v
            h = h + y
        else:
            if shared is None:
                shared = nsa_shared_kv(h, kv_norm_g, kv_w, cmp_pe, cmp_w1, cmp_w2)
            j = layer - N_A_LAYERS
            h = h + nsa_mixer(rmsnorm(h, b_norm_g[j]), shared, b_w_in[j], b_gate_b[j], b_w_o[j])
    return rmsnorm(h, final_g)
```

```python
import numpy as np
from contextlib import ExitStack
import concourse.bass as bass
import concourse.mybir as mybir
from concourse.bass_utils import run_bass_kernel_spmd

F32 = mybir.dt.float32
AF = mybir.ActivationFunctionType
ALU = mybir.AluOpType
AX = mybir.AxisListType

D = 2048
NCH = 16
NCORES = 8
HS = 64
NEG = -1.0e30


class Trk:
    __slots__ = ("w", "r", "name")

    def __init__(self, name):
        self.w = None
        self.r = {}
        self.name = name


class V:
    __slots__ = ("ap", "trk")

    def __init__(self, ap, trk):
        self.ap = ap
        self.trk = trk

    def m(self, fn):
        return V(fn(self.ap), self.trk)

    def __getitem__(self, idx):
        return V(self.ap[idx], self.trk)


class Buf:
    def __init__(self, t, name):
        self.t = t
        self.name = name
        self.trk = Trk(name)
        self.parts = {}
        self.dkey = None

    def __getitem__(self, idx):
        return V(self.t[idx], self.trk)

    def p(self, key):
        tr = self.parts.get(key)
        if tr is None:
            tr = Trk(f"{self.name}:{key}")
            self.parts[key] = tr
        return _PartView(self, tr)


class _PartView:
    def __init__(self, buf, trk):
        self.buf = buf
        self.trk = trk

    def __getitem__(self, idx):
        return V(self.buf.t[idx], self.trk)


class K:
    def __init__(self):
        self.nc = bass.Bass("TRN2", target_bir_lowering=False)
        self.es = ExitStack()
        nc = self.nc
        self.eng = {"pe": nc.tensor, "act": nc.scalar, "dve": nc.vector,
                    "pool": nc.gpsimd, "sp": nc.sync}
        self.sem = {}
        self.cnt = {}
        self.waited = {}
        self.semobj = {}
        for e in self.eng:
            self.sem[e] = self.es.enter_context(nc.semaphore("s_" + e))
            self.cnt[e] = 0
            self.waited[e] = {}
            self.semobj[("e", e)] = self.sem[e]
        self.tag = "g"
        self.out_events = []
        self.dcur = {}
        self.free_dsems = []
        self.ndsem = 0
        self.stage_es = None
        self.stage_bufs = []

    def dram(self, name, shape, kind="ExternalInput"):
        t = self.nc.dram_tensor(name, list(shape), F32, kind=kind)
        return Buf(t.ap(), name)

    def begin_stage(self, tag):
        self.stage_es = ExitStack()
        self.stage_bufs = []
        self.stage_no = getattr(self, "stage_no", 0) + 1
        self.tag = f"{tag}{self.stage_no}"

    def end_stage(self):
        need = {("e", e): self.cnt[e] for e in self.eng if self.cnt[e] > 0}
        for key, val in self.dcur.items():
            if val > 0:
                need[key] = val
        for e in self.eng:
            self._emit_waits(e, dict(need))
        for b in self.stage_bufs:
            if b.dkey is not None:
                self.free_dsems.append(b.dkey)
        self.stage_es.close()
        self.stage_es = None

    def sbuf(self, name, shape):
        st = self.stage_es if self.stage_es is not None else self.es
        t = st.enter_context(self.nc.sbuf_tensor(f"{self.tag}_{name}", list(shape), F32))
        b = Buf(t, f"{self.tag}_{name}")
        self.stage_bufs.append(b)
        return b

    def psum(self, name, shape):
        st = self.stage_es if self.stage_es is not None else self.es
        t = st.enter_context(self.nc.psum_tensor(f"{self.tag}_{name}", list(shape), F32))
        return Buf(t, f"{self.tag}_{name}")

    def _need(self, eng, reads, writes):
        need = {}

        def add(key, val):
            if key == ("e", "pe") and eng == "pe":
                return
            if key[0] == "d":
                val = self.dcur[key]
            if need.get(key, 0) < val:
                need[key] = val
        for t in reads:
            if t.w is not None:
                add(*t.w)
        for t in writes:
            if t.w is not None:
                add(*t.w)
            for key, val in t.r.items():
                add(key, val)
        return need

    def _emit_waits(self, eng, need):
        w = self.waited[eng]
        for key, val in need.items():
            if w.get(key, 0) >= val:
                continue
            self.eng[eng].wait_ge(self.semobj[key], val)
            w[key] = val

    def _mark(self, ev, reads, writes):
        key, val = ev
        for t in writes:
            t.w = ev
            t.r = {}
        for t in reads:
            if t.r.get(key, 0) < val:
                t.r[key] = val

    def op(self, eng, fn, reads, writes, *a, **kw):
        reads = [v.trk for v in reads if isinstance(v, V)]
        writes = [v.trk for v in writes if isinstance(v, V)]
        self._emit_waits(eng, self._need(eng, reads, writes))
        inst = fn(*a, **kw)
        self.cnt[eng] += 1
        inst.then_inc(self.sem[eng], 1)
        self._mark((("e", eng), self.cnt[eng]), reads, writes)
        return inst

    def dma(self, q, out, in_, owner, is_output=False, nonc=False):
        reads = [in_.trk]
        writes = [out.trk]
        self._emit_waits(q, self._need(q, reads, writes))
        if owner.dkey is None:
            if self.free_dsems:
                owner.dkey = self.free_dsems.pop()
            else:
                key = ("d", self.ndsem)
                self.ndsem += 1
                self.semobj[key] = self.es.enter_context(self.nc.semaphore(f"dq{key[1]}"))
                self.dcur[key] = 0
                owner.dkey = key
        key = owner.dkey
        if nonc:
            with self.nc.allow_non_contiguous_dma(reason="tiny strided column transfer"):
                inst = self.eng[q].dma_start(out=out.ap, in_=in_.ap)
        else:
            inst = self.eng[q].dma_start(out=out.ap, in_=in_.ap)
        self.dcur[key] += 16
        inst.then_inc(self.semobj[key], 16)
        ev = (key, self.dcur[key])
        self._mark(ev, reads, writes)
        if is_output:
            self.out_events.append(ev)

    def load(self, sb, dr, q="sp"):
        self.dma(q, sb[:] if isinstance(sb, Buf) else sb, dr[:] if isinstance(dr, Buf) else dr,
                 sb if isinstance(sb, Buf) else None)

    @staticmethod
    def _a(x):
        return x.ap if isinstance(x, V) else x

    def tt(self, out, in0, in1, op, eng="dve"):
        e = self.eng[eng]
        return self.op(eng, e.tensor_tensor, [in0, in1], [out], out=out.ap, in0=in0.ap, in1=in1.ap, op=op)

    def ts(self, out, in0, s1, op0, s2=None, op1=None, eng="dve", accum_out=None):
        e = self.eng[eng]
        kw = dict(out=out.ap, in0=in0.ap, scalar1=self._a(s1), scalar2=self._a(s2), op0=op0)
        if op1 is not None:
            kw["op1"] = op1
        w = [out]
        if accum_out is not None:
            kw["accum_out"] = accum_out.ap
            w.append(accum_out)
        return self.op(eng, e.tensor_scalar, [in0, s1, s2], w, **kw)

    def stt(self, out, in0, s, in1, op0, op1):
        return self.op("dve", self.nc.vector.scalar_tensor_tensor, [in0, s, in1], [out],
                       out=out.ap, in0=in0.ap, scalar=self._a(s), in1=in1.ap, op0=op0, op1=op1)

    def act(self, out, in_, func, bias=None, scale=None, accum_out=None):
        kw = dict(out=out.ap, in_=in_.ap, func=func)
        if bias is not None:
            kw["bias"] = self._a(bias)
        if scale is not None:
            kw["scale"] = self._a(scale)
        w = [out]
        if accum_out is not None:
            kw["accum_out"] = accum_out.ap
            w.append(accum_out)
        return self.op("act", self.nc.scalar.activation, [in_, bias, scale], w, **kw)

    def copy(self, out, in_, eng="act"):
        if eng == "act":
            return self.op("act", self.nc.scalar.copy, [in_], [out], out=out.ap, in_=in_.ap)
        return self.op(eng, self.eng[eng].tensor_copy, [in_], [out], out=out.ap, in_=in_.ap)

    def mm(self, out, lhsT, rhs, start=True, stop=True):
        return self.op("pe", self.nc.tensor.matmul, [lhsT, rhs], [out], out.ap, lhsT.ap, rhs.ap,
                       start=start, stop=stop)

    def tr(self, out, in_, ident):
        return self.op("pe", self.nc.tensor.transpose, [in_, ident], [out], out.ap, in_.ap, ident.ap)

    def memset(self, out, val, eng="pool"):
        return self.op(eng, self.eng[eng].memset, [], [out], out.ap, val)

    def finish(self):
        need = {}
        for key, val in self.out_events:
            if need.get(key, 0) < val:
                need[key] = val
        for e in self.eng:
            if e != "sp" and self.cnt[e] > 0:
                need[("e", e)] = self.cnt[e]
        self._emit_waits("sp", need)
        self.es.close()
        return self.nc


FM = lambda a: a.rearrange("(c p) t -> p c t", p=128)


def rmsnorm_fm(k, x, u, gvec, ones, ps, sq, rstd, ncols):
    for c in range(NCH):
        k.act(sq.p(c)[:, c, :ncols], x[:, c, :ncols], AF.Square)
    for c in range(NCH):
        k.mm(ps[:, :ncols], ones[:], sq.p(c)[:, c, :ncols], start=(c == 0), stop=(c == NCH - 1))
    k.ts(rstd[:, :ncols], ps[:, :ncols], 1.0 / D, ALU.mult, 1e-6, ALU.add)
    k.act(rstd[:, :ncols], rstd[:, :ncols], AF.Ln)
    k.act(rstd[:, :ncols], rstd[:, :ncols], AF.Exp, scale=-0.5)
    for c in range(NCH):
        k.stt(u.p(c)[:, c, :ncols], x[:, c, :ncols], gvec[:, c:c + 1], rstd[:, :ncols], ALU.mult, ALU.mult)


A_OUTS = ["atil", "rtil", "btil", "ktil", "vv", "sz", "bonus"]
VEC_A = ["ng", "mu0", "mu1", "mu2", "mu3", "mu4", "mu5", "w0", "a0", "v0", "k_k", "k_a", "r_k"]


def stage_rwkv_pre(k, T, vres, hT, W, C, outs, pc, vf):
    k.begin_stage("A")
    TT = 128
    NT = T // TT
    vecs = k.sbuf("vecs", [128, len(VEC_A), 16])
    negw0 = k.sbuf("negw0", [128, 16])
    w1s = k.sbuf("w1s", [128, 16, 96]); w2s = k.sbuf("w2s", [96, D])
    a1s = k.sbuf("a1s", [128, 16, 96]); a2s = k.sbuf("a2s", [96, D])
    v1s = k.sbuf("v1s", [128, 16, 64]); v2s = k.sbuf("v2s", [64, D])
    bds = k.sbuf("bds", [128, 128]); ones = k.sbuf("oness", [128, 128]); rms = k.sbuf("rms", [128, TT])
    lst = [(vecs, W["vec"]), (w1s, W["w1"]), (w2s, W["w2"]), (a1s, W["a1"]), (a2s, W["a2"]),
           (bds, C["bd"]), (ones, C["ones"]), (rms, C["rmask"])]
    if vres:
        lst += [(v1s, W["v1"]), (v2s, W["v2"])]
    for sb, dr in lst:
        k.load(sb, dr)
    wq = W["wq"]
    vi = {n: i for i, n in enumerate(VEC_A)}
    vcol = lambda n, c: vecs[:, vi[n], c:c + 1]
    k.ts(negw0[:], vecs[:, vi["w0"], :], -1.0, ALU.mult)

    ht = k.sbuf("ht", [128, 16, TT + 1])
    sq = k.sbuf("sq", [128, 16, TT + 1])
    u = k.sbuf("u", [128, 16, TT + 1])
    xx = sq
    rstd = k.sbuf("rstd", [128, TT + 1])
    mix = k.sbuf("mix", [128, 4, 16, TT])
    wb = [k.sbuf(f"wb{i}", [128, 16, 128]) for i in range(6)]
    h1w = k.sbuf("h1w", [96, TT]); h1a = k.sbuf("h1a", [96, TT]); h1v = k.sbuf("h1v", [64, TT])
    ps_n = k.psum("ps_n", [128, 512])
    ps_p = [k.psum(f"ps_p{i}", [128, 512]) for i in range(4)]
    ps_l = k.psum("ps_l", [128, 512])
    ps_m = k.psum("ps_m", [128, 512])
    ps_h = k.psum("ps_h", [128, 512])
    E = {}
    for n in ["r", "kx", "v", "sz", "e", "a", "sv", "kkr", "t1", "t2", "km", "cum", "p", "pinv", "pprev",
              "o_a", "o_r", "o_b", "o_k", "o_bn", "vft"]:
        nb = 2 if (n.startswith("o_") or n in ("sz", "v", "p")) else 1
        E[n] = [k.sbuf(f"e_{n}{i}", [128, TT]) for i in range(nb)]

    wcount = 0
    for t in range(NT):
        t0 = t * TT
        k.load(ht, hT[:, t0:t0 + TT + 1].m(FM))
        rmsnorm_fm(k, ht, u, vecs[:, vi["ng"], :], ones, ps_n, sq, rstd, TT + 1)
        for c in range(NCH):
            k.tt(xx.p(c)[:, c, 0:TT], u.p(c)[:, c, 0:TT], u.p(c)[:, c, 1:TT + 1], ALU.subtract)
        for i in range(4):
            for c in range(NCH):
                k.stt(mix.p((i, c))[:, i, c, :], xx.p(c)[:, c, 0:TT], vcol(f"mu{i}", c),
                      u.p(c)[:, c, 1:TT + 1], ALU.mult, ALU.add)
        for (ws, mi, dst, nr, fn) in [(w1s, 4, h1w, 96, AF.Tanh), (a1s, 5, h1a, 96, None), (v1s, 2, h1v, 64, None)]:
            if ws is v1s and not vres:
                continue
            if mi >= 4:
                for c in range(NCH):
                    k.stt(ht[:, c, 0:TT], xx.p(c)[:, c, 0:TT], vcol(f"mu{mi}", c),
                          u.p(c)[:, c, 1:TT + 1], ALU.mult, ALU.add)
            for c in range(NCH):
                src = ht[:, c, 0:TT] if mi >= 4 else mix.p((mi, c))[:, mi, c, :]
                k.mm(ps_h[:nr, :TT], ws[:, c, :], src, start=(c == 0), stop=(c == NCH - 1))
            if fn is None:
                k.copy(dst[:nr, :], ps_h[:nr, :TT])
            else:
                k.act(dst[:nr, :], ps_h[:nr, :TT], fn)
        for oc in range(NCH):
            w = []
            for pj in range(4):
                wbuf = wb[wcount % 6]
                wcount += 1
                k.dma("sp", wbuf[:], wq[pj, oc, :, :, :], wbuf)
                w.append(wbuf)
            e = {n: E[n][oc % len(E[n])] for n in E}
            osl = slice(oc * 128, (oc + 1) * 128)
            if vres:
                k.dma("sp", e["vft"][:], vf[osl, t0:t0 + TT], e["vft"])
            pcol = (oc % 2) * 128
            PP = [ps_p[pj].p(oc % 2)[:, pcol:pcol + TT] for pj in range(4)]
            for pj in range(4):
                for c in range(NCH):
                    k.mm(PP[pj], w[pj][:, c, :], mix.p((pj, c))[:, pj, c, :],
                         start=(c == 0), stop=(c == NCH - 1))
            k.mm(ps_l[:, 0:TT], w2s[:, osl], h1w[:, :])
            k.mm(ps_l[:, 128:128 + TT], a2s[:, osl], h1a[:, :])
            if vres:
                k.mm(ps_l[:, 256:256 + TT], v2s[:, osl], h1v[:, :])
            r, kx, v, a = e["r"], e["kx"], e["v"], e["a"]
            k.copy(r[:], PP[0])
            k.copy(kx[:], PP[1])
            k.copy(v[:], PP[2], eng="dve")
            k.act(e["sz"][:], PP[3], AF.Silu)
            k.dma("pool", outs["sz"][osl, t0:t0 + TT], e["sz"][:], e["sz"])
            if vres:
                k.act(e["sv"][:], ps_l[:, 256:256 + TT], AF.Sigmoid, bias=vcol("v0", oc))
                k.tt(e["t1"][:], e["vft"][:], v[:], ALU.subtract)
                k.tt(e["t1"][:], e["t1"][:], e["sv"][:], ALU.mult)
                k.tt(v[:], v[:], e["t1"][:], ALU.add)
            k.dma("pool", outs["vv"][osl, t0:t0 + TT], v[:], v)
            k.act(e["t2"][:], ps_l[:, 0:TT], AF.Exp, bias=negw0[:, oc:oc + 1], scale=-1.0)
            k.act(e["t2"][:], e["t2"][:], AF.Ln, bias=1.0)
            k.act(e["e"][:], e["t2"][:], AF.Exp, bias=-0.5, scale=-1.0)
            k.act(a[:], ps_l[:, 128:128 + TT], AF.Sigmoid, bias=vcol("a0", oc))
            k.ts(e["kkr"][:], kx[:], vcol("k_k", oc), ALU.mult)
            k.tt(e["t1"][:], e["kkr"][:], e["kkr"][:], ALU.mult)
            k.mm(ps_m[:, 0:TT], bds[:], e["t1"][:])
            k.ts(e["t1"][:], ps_m[:, 0:TT], 1e-24, ALU.max)
            k.act(e["t1"][:], e["t1"][:], AF.Ln)
            k.act(e["t1"][:], e["t1"][:], AF.Exp, scale=-0.5)
            k.tt(e["kkr"][:], e["kkr"][:], e["t1"][:], ALU.mult)
            k.ts(e["t1"][:], a[:], -1.0, ALU.add, vcol("k_a", oc), ALU.mult)
            k.stt(e["km"][:], e["t1"][:], 1.0, kx[:], ALU.add, ALU.mult)
            k.stt(e["t1"][:], r[:], vcol("r_k", oc), e["km"][:], ALU.mult, ALU.mult)
            k.mm(ps_m[:, 128:128 + TT], bds[:], e["t1"][:])
            k.tt(e["o_bn"][:], ps_m[:, 128:128 + TT], v[:], ALU.mult)
            k.dma("pool", outs["bonus"][osl, t0:t0 + TT], e["o_bn"][:], e["o_bn"])
            k.op("dve", k.nc.vector.tensor_tensor_scan, [rms[:], e["e"][:]], [e["cum"][:]],
                 out=e["cum"][:].ap, data0=rms[:].ap, data1=e["e"][:].ap, initial=0.0,
                 op0=ALU.mult, op1=ALU.add)
            k.act(e["p"][:], e["cum"][:], AF.Exp, scale=-1.0)
            k.act(e["pinv"][:], e["cum"][:], AF.Exp)
            k.tt(e["t1"][:], e["cum"][:], e["e"][:], ALU.subtract)
            k.act(e["pprev"][:], e["t1"][:], AF.Exp, scale=-1.0)
            k.dma("pool", pc[osl, 2 * t:2 * t + 2], e["p"][:, 63:TT:64], e["p"], nonc=True)
            k.stt(e["o_a"][:], e["kkr"][:], -1.0, e["pprev"][:], ALU.mult, ALU.mult)
            k.dma("pool", outs["atil"][osl, t0:t0 + TT], e["o_a"][:], e["o_a"])
            k.tt(e["o_r"][:], r[:], e["p"][:], ALU.mult)
            k.dma("pool", outs["rtil"][osl, t0:t0 + TT], e["o_r"][:], e["o_r"])
            k.tt(e["t1"][:], e["kkr"][:], a[:], ALU.mult)
            k.tt(e["o_b"][:], e["t1"][:], e["pinv"][:], ALU.mult)
            k.dma("pool", outs["btil"][osl, t0:t0 + TT], e["o_b"][:], e["o_b"])
            k.tt(e["o_k"][:], e["km"][:], e["pinv"][:], ALU.mult)
            k.dma("pool", outs["ktil"][osl, t0:t0 + TT], e["o_k"][:], e["o_k"])
    k.end_stage()


def stage_scan(k, T, A, pc, C, o_tok):
    k.begin_stage("B")
    NC_ = T // 64
    NH = 16
    GH = 8
    pcs = k.sbuf("pcs", [64, NH, NC_]); mn = k.sbuf("mn", [64, GH, 128])
    ml = k.sbuf("ml", [64, GH, 64]); idt = k.sbuf("idt", [64, GH, 64])
    for sb, dr in [(mn, C["MN"]), (ml, C["ML"]), (idt, C["ID"])]:
        k.load(sb, dr)
    H = k.sbuf("H", [64, NH, 64])
    NBUF = 2
    ar = [k.sbuf(f"ar{i}", [64, NH, 2, 64]) for i in range(NBUF)]
    fb = [k.sbuf(f"fb{i}", [64, NH, 64]) for i in range(NBUF)]
    fk = [k.sbuf(f"fk{i}", [64, NH, 64]) for i in range(NBUF)]
    fv = [k.sbuf(f"fv{i}", [64, NH, 64]) for i in range(NBUF)]
    tb = [k.sbuf(f"tb{i}", [64, NH, 64]) for i in range(NBUF)]
    tk = [k.sbuf(f"tk{i}", [64, NH, 64]) for i in range(NBUF)]
    tv = [k.sbuf(f"tv{i}", [64, NH, 64]) for i in range(NBUF)]
    osb = [k.sbuf(f"osb{i}", [64, NH, 64]) for i in range(2)]
    nb = [k.sbuf(f"nb{i}", [64, GH, 128]) for i in range(2)]
    nk = [k.sbuf(f"nk{i}", [64, GH, 128]) for i in range(2)]
    Ls = [[k.sbuf(f"Ls{g}_{i}", [64, GH, 64]) for i in range(2)] for g in range(2)]
    Ns = [[k.sbuf(f"Ns{g}_{i}", [64, GH, 64]) for i in range(2)] for g in range(2)]
    Pm = [[k.sbuf(f"Pm{g}_{i}", [64, GH, 64]) for i in range(2)] for g in range(2)]
    xs = [k.sbuf(f"xs{g}", [64, GH, 64]) for g in range(2)]
    us = [k.sbuf(f"us{g}", [64, GH, 64]) for g in range(2)]
    htmp = [k.sbuf(f"htmp{g}", [64, GH, 64]) for g in range(2)]
    pb = [k.psum(f"pb{i}", [128, 512]) for i in range(8)]
    J8 = lambda a: a.rearrange("p (j c) -> p j c", j=GH)
    J4 = lambda a: a.rearrange("p (j c) -> p j c", j=4)
    it = 0
    tcnt = 0
    for hf in range(2):
        F0 = hf * 1024
        fm = lambda name, n: A[name][F0:F0 + 1024, n * 64:(n + 1) * 64].m(lambda a: a.rearrange("(h k) t -> k h t", k=64))
        k.load(pcs, pc[F0:F0 + 1024, :].m(lambda a: a.rearrange("(h k) n -> k h n", k=64)))
        for g in range(NH // GH):
            k.memset(H.p(g)[:, g * GH:(g + 1) * GH, :], 0.0)
        for n in range(NC_):
            bi = n % NBUF
            k.dma("sp", ar[bi].p(0)[:, :, 0, :], fm("atil", n), ar[bi])
            k.dma("sp", ar[bi].p(0)[:, :, 1, :], fm("rtil", n), ar[bi])
            k.load(fb[bi], fm("btil", n))
            k.load(fk[bi], fm("ktil", n))
            k.load(fv[bi], fm("vv", n))
            a_ = ar[bi].p(0)
            for (src, dst) in [(fb[bi], tb[bi]), (fk[bi], tk[bi]), (fv[bi], tv[bi])]:
                for g in range(NH // GH):
                    bank = pb[5 + tcnt % 3]
                    tcnt += 1
                    for j in range(GH):
                        k.tr(bank[:64, j * 64:(j + 1) * 64], src[:, g * GH + j, :], idt[:, 0, :])
                    k.copy(dst.p(g)[:, g * GH:(g + 1) * GH, :], bank[:64, :].m(J8), eng=("act" if tcnt % 2 else "dve"))
            grp = list(range(NH // GH))
            R_ = lambda a: a.rearrange("p a t -> p (a t)")
            stt_ = {}
            for g in grp:
                NB, NK = nb[g], nk[g]
                hs = [g * GH + j for j in range(GH)]
                for j, h in enumerate(hs):
                    bank, col = pb[j // 4], (j % 4) * 128
                    k.mm(bank[:64, col:col + 128], fb[bi][:, h, :], a_[:, h, :, :].m(R_))
                for j, h in enumerate(hs):
                    bank, col = pb[2 + j // 4], (j % 4) * 128
                    k.mm(bank[:64, col:col + 128], fk[bi][:, h, :], a_[:, h, :, :].m(R_))
                for j, h in enumerate(hs):
                    k.mm(pb[4][:64, j * 64:(j + 1) * 64], a_[:, h, 0, :], fb[bi][:, h, :])
                for half in range(2):
                    k.tt(NB[:, half * 4:(half + 1) * 4, :], pb[half][:64, :].m(J4), mn[:, half * 4:(half + 1) * 4, :], ALU.mult)
                    k.tt(NK[:, half * 4:(half + 1) * 4, :], pb[2 + half][:64, :].m(J4), mn[:, half * 4:(half + 1) * 4, :], ALU.mult)
                Lc = Ls[g][0]
                k.tt(Lc[:], pb[4][:64, :].m(J8), ml[:], ALU.mult)
                Pc = Pm[g][0]
                k.tt(Pc[:], NB[:, :, 0:64], idt[:], ALU.add)
                stt_[g] = {"NB": NB, "NK": NK, "Lc": Lc, "Nc": None, "Pc": Pc, "hs": hs}
            LB = {0: (pb[5], pb[6], pb[7]), 1: (pb[0], pb[1], pb[2])}
            for lvl in range(5):
                for g in grp:
                    S_ = stt_[g]
                    bL, bN, bP = LB[g]
                    Nv = (lambda j, NB=S_["NB"]: NB[:, j, 0:64]) if S_["Nc"] is None else (lambda j, Nc=S_["Nc"]: Nc[:, j, :])
                    for j in range(GH):
                        k.mm(bL[:64, j * 64:(j + 1) * 64], Nv(j), S_["Lc"][:, j, :])
                    if lvl < 4:
                        for j in range(GH):
                            k.mm(bN[:64, j * 64:(j + 1) * 64], S_["Lc"][:, j, :], Nv(j))
                for g in grp:
                    S_ = stt_[g]
                    bL, bN, bP = LB[g]
                    Ln = Ls[g][(lvl + 1) % 2]
                    k.copy(Ln[:], bL[:64, :].m(J8))
                    if lvl < 4:
                        Nn = Ns[g][lvl % 2]
                        k.copy(Nn[:], bN[:64, :].m(J8), eng="dve")
                        S_["Nc"] = Nn
                    S_["Lc"] = Ln
                for g in grp:
                    S_ = stt_[g]
                    bL, bN, bP = LB[g]
                    for j in range(GH):
                        k.mm(bP[:64, j * 64:(j + 1) * 64], S_["Lc"][:, j, :], S_["Pc"][:, j, :])
                for g in grp:
                    S_ = stt_[g]
                    bL, bN, bP = LB[g]
                    Pn = Pm[g][(lvl + 1) % 2]
                    k.tt(Pn[:], bP[:64, :].m(J8), S_["Pc"][:], ALU.add)
                    S_["Pc"] = Pn
            BX = {0: pb[3], 1: pb[4]}; BU = {0: pb[5], 1: pb[6]}; BO = {0: pb[7], 1: pb[0]}; BH = {0: pb[1], 1: pb[2]}
            ob = osb[n % 2]
            for g in grp:
                S_ = stt_[g]; Hg = H.p(g); tvg = tv[bi].p(g)
                for j, h in enumerate(S_["hs"]):
                    k.mm(BX[g][:64, j * 64:(j + 1) * 64], a_[:, h, 0, :], Hg[:, h, :], start=True, stop=False)
                    k.mm(BX[g][:64, j * 64:(j + 1) * 64], S_["NK"][:, j, 0:64], tvg[:, h, :], start=False, stop=True)
            for g in grp:
                k.copy(xs[g][:], BX[g][:64, :].m(J8), eng=("act" if g == 0 else "dve"))
            for g in grp:
                S_ = stt_[g]
                for j, h in enumerate(S_["hs"]):
                    k.mm(BU[g][:64, j * 64:(j + 1) * 64], S_["Pc"][:, j, :], xs[g][:, j, :])
            for g in grp:
                k.copy(us[g][:], BU[g][:64, :].m(J8), eng=("dve" if g == 0 else "act"))
            for g in grp:
                S_ = stt_[g]; Hg = H.p(g); tvg = tv[bi].p(g)
                for j, h in enumerate(S_["hs"]):
                    k.mm(BO[g][:64, j * 64:(j + 1) * 64], a_[:, h, 1, :], Hg[:, h, :], start=True, stop=False)
                    k.mm(BO[g][:64, j * 64:(j + 1) * 64], S_["NB"][:, j, 64:128], us[g][:, j, :], start=False, stop=False)
                    k.mm(BO[g][:64, j * 64:(j + 1) * 64], S_["NK"][:, j, 64:128], tvg[:, h, :], start=False, stop=True)
            for g in grp:
                k.copy(ob.p(g)[:, g * GH:(g + 1) * GH, :], BO[g][:64, :].m(J8), eng=("act" if g == 0 else "dve"))
            for g in grp:
                S_ = stt_[g]; tbg, tkg, tvg = tb[bi].p(g), tk[bi].p(g), tv[bi].p(g)
                for j, h in enumerate(S_["hs"]):
                    k.mm(BH[g][:64, j * 64:(j + 1) * 64], tbg[:, h, :], us[g][:, j, :], start=True, stop=False)
                    k.mm(BH[g][:64, j * 64:(j + 1) * 64], tkg[:, h, :], tvg[:, h, :], start=False, stop=True)
            for g in grp:
                Hg = H.p(g)
                k.tt(htmp[g][:], BH[g][:64, :].m(J8), Hg[:, g * GH:(g + 1) * GH, :], ALU.add)
                k.tt(Hg[:, g * GH:(g + 1) * GH, :], htmp[g][:],
                     pcs[:, g * GH:(g + 1) * GH, n:n + 1].m(lambda a: a.broadcast_to([64, GH, 64])), ALU.mult)
            ob = osb[n % 2]
            for g in range(NH // GH):
                k.dma("pool", o_tok[n * 64:(n + 1) * 64, F0 + g * 512:F0 + (g + 1) * 512].m(
                    lambda a: a.rearrange("s (h v) -> s h v", v=64)), ob.p(g)[:, g * GH:(g + 1) * GH, :], ob)
    k.end_stage()


def stage_post(k, T, mode, o_tok, szT, hT, bnT, vec, wo, C, hn, hn_off, is_output):
    k.begin_stage("C")
    TT = 256
    NT = T // TT
    rw = mode == "rwkv"
    fin = mode == "nsa_final"
    hres, hoff = hT
    vecs = k.sbuf("vecs", [128, 3, 16]); bds = k.sbuf("bds", [128, 128]); ones = k.sbuf("oness", [128, 128])
    idn = k.sbuf("idn", [128, 128])
    for sb, dr in [(vecs, vec), (bds, C["bdm"]), (ones, C["ones"]), (idn, C["IDN"])]:
        k.load(sb, dr)
    ot = k.sbuf("ot", [128, 2, D])
    o = k.sbuf("o", [128, 16, TT]); sz = k.sbuf("sz", [128, 16, TT]); h = k.sbuf("h", [128, 16, TT])
    bn = k.sbuf("bn", [128, 16, TT]) if rw else None
    ybuf = [k.sbuf(f"y{i}", [128, 16, TT]) for i in range(2)]
    hnew = k.sbuf("hnew", [128, 16, TT])
    wb = [k.sbuf(f"wb{i}", [128, 16, 128]) for i in range(3)]
    t1 = k.sbuf("t1", [128, TT]); t2 = k.sbuf("t2", [128, TT]); rstd = k.sbuf("rstd", [128, TT])
    pm = k.psum("pm", [128, 512]); pv = k.psum("pv", [128, 512])
    po = [k.psum(f"po{i}", [128, 512]) for i in range(2)]
    pn = k.psum("pn", [128, 512])
    pt = [k.psum(f"pt{i}", [128, 512]) for i in range(2)]
    wc = 0
    for t in range(NT):
        sl = slice(t * TT, (t + 1) * TT)
        y = ybuf[t % 2]
        for hh in range(2):
            k.dma("sp", ot.p(hh)[:, hh, :], o_tok[t * TT + hh * 128:t * TT + (hh + 1) * 128, :], ot)
        k.load(sz, szT[:, sl].m(FM))
        k.load(h, hres[:, hoff + t * TT:hoff + (t + 1) * TT].m(FM))
        if rw:
            k.load(bn, bnT[:, sl].m(FM))
        for c in range(NCH):
            ptc = pt[c % 2]
            for hh in range(2):
                k.tr(ptc[:, hh * 128:(hh + 1) * 128], ot.p(hh)[:, hh, c * 128:(c + 1) * 128], idn[:])
            k.copy(o.p(c)[:, c, :], ptc[:, :TT], eng=("act" if c % 2 else "dve"))
        for c in range(NCH):
            yc = y.p(c)[:, c, :]
            oc_ = o.p(c)[:, c, :]
            if rw:
                k.mm(pm[:, :TT], bds[:], oc_)
                k.tt(yc, oc_, pm[:, :TT], ALU.subtract)
                k.act(t1[:], yc, AF.Square)
                k.mm(pv[:, :TT], bds[:], t1[:])
                k.ts(t2[:], pv[:, :TT], 64e-5, ALU.add)
                k.act(t2[:], t2[:], AF.Ln)
                k.act(t2[:], t2[:], AF.Exp, scale=-0.5)
                k.tt(yc, yc, t2[:], ALU.mult)
                k.ts(yc, yc, vecs[:, 0, c:c + 1], ALU.mult, vecs[:, 1, c:c + 1], ALU.add)
                k.tt(yc, yc, bn[:, c, :], ALU.add)
                k.tt(yc, yc, sz[:, c, :], ALU.mult)
            else:
                k.tt(yc, oc_, sz[:, c, :], ALU.mult)
        for oc in range(NCH):
            w = wb[wc % 3]
            wc += 1
            k.load(w, wo[oc])
            ps = po[oc % 2]
            for c in range(NCH):
                k.mm(ps[:, :TT], w[:, c, :], y.p(c)[:, c, :], start=(c == 0), stop=(c == NCH - 1))
            k.tt(hnew.p(oc)[:, oc, :], ps[:, :TT], h[:, oc, :], ALU.add)
        if fin:
            for c in range(NCH):
                k.act(o.p(c)[:, c, :], hnew.p(c)[:, c, :], AF.Square)
            for c in range(NCH):
                k.mm(pn[:, :TT], ones[:], o.p(c)[:, c, :], start=(c == 0), stop=(c == NCH - 1))
            k.ts(rstd[:], pn[:, :TT], 1.0 / D, ALU.mult, 1e-6, ALU.add)
            k.act(rstd[:], rstd[:], AF.Ln)
            k.act(rstd[:], rstd[:], AF.Exp, scale=-0.5)
            for c in range(NCH):
                k.stt(o.p(c)[:, c, :], hnew.p(c)[:, c, :], vecs[:, 2, c:c + 1], rstd[:], ALU.mult, ALU.mult)
            for c in range(NCH):
                k.dma("pool", hn[c * 128:(c + 1) * 128, hn_off + t * TT:hn_off + (t + 1) * TT], o.p(c)[:, c, :], o,
                      is_output=is_output)
        else:
            for c in range(NCH):
                k.dma("pool", hn[c * 128:(c + 1) * 128, hn_off + t * TT:hn_off + (t + 1) * TT], hnew.p(c)[:, c, :], hnew,
                      is_output=is_output)
    k.end_stage()


def stage_proj(k, T, kinds, hT, gv, W, wg, gb, ct, st, C, Y, G):
    k.begin_stage("P")
    TT = 256
    NT = T // TT
    NOC = len(kinds)
    hres, hoff = hT
    with_gates = wg is not None
    gvs = k.sbuf("gvs", [128, 16]); wgs = k.sbuf("wgs", [128, 16, 48]); gbs = k.sbuf("gbs", [48, 1])
    cts = k.sbuf("cts", [128, T]); sts = k.sbuf("sts", [128, T]); rts = k.sbuf("rts", [128, 128])
    ones = k.sbuf("oness", [128, 128])
    lst = [(gvs, gv), (cts, ct), (sts, st), (rts, C["rt"]), (ones, C["ones"])]
    if with_gates:
        lst += [(wgs, wg), (gbs, gb)]
    for sb, dr in lst:
        k.load(sb, dr)
    h = k.sbuf("h", [128, 16, TT]); sq = k.sbuf("sq", [128, 16, TT])
    ubuf = [k.sbuf(f"u{i}", [128, 16, TT]) for i in range(2)]
    rstd = k.sbuf("rstd", [128, TT])
    wb = [k.sbuf(f"wb{i}", [128, 16, 128]) for i in range(3)]
    ys = [k.sbuf(f"ys{i}", [128, TT]) for i in range(2)]
    qs = k.sbuf("qs", [128, TT]); t1 = k.sbuf("t1", [128, TT]); gs = k.sbuf("gsb", [48, TT])
    pn = k.psum("pn", [128, 512]); pr = k.psum("pr", [128, 512]); pg = k.psum("pg", [128, 512])
    po = [k.psum(f"po{i}", [128, 512]) for i in range(2)]
    wc = 0
    for t in range(NT):
        sl = slice(t * TT, (t + 1) * TT)
        u = ubuf[t % 2]
        k.load(h, hres[:, hoff + t * TT:hoff + (t + 1) * TT].m(FM))
        rmsnorm_fm(k, h, u, gvs[:, :], ones, pn, sq, rstd, TT)
        if with_gates:
            for c in range(NCH):
                k.mm(pg[:48, :TT], wgs[:, c, :], u.p(c)[:, c, :], start=(c == 0), stop=(c == NCH - 1))
            k.act(gs[:], pg[:48, :TT], AF.Sigmoid, bias=gbs[:, 0:1])
            k.dma("pool", G[:, sl], gs[:], gs)
        for oc in range(NOC):
            w = wb[wc % 3]
            k.load(w, W[oc])
            ps = po[wc % 2]
            y = ys[wc % 2]
            wc += 1
            for c in range(NCH):
                k.mm(ps[:, :TT], w[:, c, :], u.p(c)[:, c, :], start=(c == 0), stop=(c == NCH - 1))
            kind = kinds[oc]
            if kind == "copy":
                k.copy(y[:], ps[:, :TT])
            elif kind == "silu":
                k.act(y[:], ps[:, :TT], AF.Silu)
            else:
                k.copy(qs[:], ps[:, :TT])
                k.mm(pr[:, :TT], rts[:], qs[:])
                k.tt(t1[:], qs[:], cts[:, sl], ALU.mult)
                k.tt(y[:], pr[:, :TT], sts[:, sl], ALU.mult)
                k.tt(y[:], y[:], t1[:], ALU.add)
            k.dma("pool", Y[oc * 128:(oc + 1) * 128, sl], y[:], y)
    k.end_stage()


def stage_cmp(k, T, KV, PE, W1, W2, ct, st, C, KCo, VCo):
    k.begin_stage("M")
    NCP = T // 16
    NCM = NCP - 1
    CW = min(128, NCP)
    NCC = NCP // CW
    pes = k.sbuf("pes", [128, 2, 32]); w1s = k.sbuf("w1s", [128, 2, 32, 256]); w2s = k.sbuf("w2s", [128, 2, 2, 128])
    cts = k.sbuf("cts", [128, NCP]); sts = k.sbuf("sts", [128, NCP]); rts = k.sbuf("rts", [128, 128])
    idn = k.sbuf("idn", [128, 128])
    for j in range(2):
        k.dma("sp", pes.p(j)[:, j, :], PE[j], pes)
        k.dma("sp", w1s.p(j)[:, j, :, :], W1[j], w1s)
        k.dma("sp", w2s.p(j)[:, j, :, :], W2[j], w2s)
    for sb, dr in [(cts, ct), (sts, st), (rts, C["rt"]), (idn, C["IDN"])]:
        k.load(sb, dr)
    xc = [k.sbuf(f"xc{i}", [128, T]) for i in range(2)]
    xl = [k.sbuf(f"xl{i}", [128, NCP]) for i in range(3)]
    hid = k.sbuf("hid", [128, 2, NCP]); ob = [k.sbuf(f"ob{i}", [128, NCP]) for i in range(2)]
    vt = [k.sbuf(f"vt{i}", [CW, NCC, 128]) for i in range(2)]
    qs = k.sbuf("qs", [128, NCP]); t1 = k.sbuf("t1", [128, NCP])
    p1 = [k.psum(f"p1{i}", [128, 512]) for i in range(2)]
    p2 = k.psum("p2", [128, 512]); pr = k.psum("pr", [128, 512])
    it = 0
    for g in range(4):
        for j in range(2):
            x = xc[it % 2]
            o = ob[it % 2]
            it += 1
            k.load(x, KV[(j * 4 + g) * 128:(j * 4 + g + 1) * 128, :])
            for l in range(32):
                xb = xl[l % 3]
                src = x[:, l:l + 16 * (NCM - 1) + 1:16]
                k.ts(xb[:, :NCM], src, pes.p(j)[:, j, l:l + 1], ALU.add)
                for hc in range(2):
                    k.mm(p1[hc][:, :NCM], w1s.p(j)[:, j, l, hc * 128:(hc + 1) * 128], xb[:, :NCM],
                         start=(l == 0), stop=(l == 31))
            for hc in range(2):
                k.act(hid[:, hc, :NCM], p1[hc][:, :NCM], AF.Silu)
            for hc in range(2):
                k.mm(p2[:, :NCM], w2s.p(j)[:, j, hc, :], hid[:, hc, :NCM], start=(hc == 0), stop=(hc == 1))
            k.memset(o[:], 0.0)
            if j == 0:
                k.copy(qs[:, :NCM], p2[:, :NCM])
                k.mm(pr[:, :NCM], rts[:], qs[:, :NCM])
                k.tt(t1[:, :NCM], qs[:, :NCM], cts[:, :NCM], ALU.mult)
                k.tt(o[:, :NCM], pr[:, :NCM], sts[:, :NCM], ALU.mult)
                k.tt(o[:, :NCM], o[:, :NCM], t1[:, :NCM], ALU.add)
                k.dma("pool", KCo[g], o[:], o)
            else:
                k.copy(o[:, :NCM], p2[:, :NCM])
                v_ = vt[g % 2]
                for cc in range(NCC):
                    k.tr(pr[:CW, cc * 128:(cc + 1) * 128], o[:, cc * CW:(cc + 1) * CW], idn[:])
                for cc in range(NCC):
                    k.copy(v_[:, cc, :], pr[:CW, cc * 128:(cc + 1) * 128])
                k.dma("pool", VCo[g], v_[:], v_)
    k.end_stage()


def stage_attn(k, T, Y, G, KV, KCo, VCo, C, o_tok):
    k.begin_stage("T")
    nc = k.nc
    NQ = T // 128
    NBLK = T // 64
    NCP = T // 16
    CW = min(128, NCP)
    NCC = NCP // CW
    ovs = k.sbuf("ovs", [CW, NCC, NBLK]); band = k.sbuf("band", [128, 16]); rv = k.sbuf("rv", [128, 1])
    keep = k.sbuf("keep", [128, NQ, NBLK]); addc = k.sbuf("addc", [128, NQ, NBLK])
    caus = k.sbuf("caus", [128, 128]); wlo = k.sbuf("wlo", [128, 128]); idn = k.sbuf("idn", [128, 128])
    for sb, dr in [(ovs, C["OV"]), (band, C["BAND"]), (rv, C["RV"]), (keep, C["KEEP"]), (addc, C["ADDC"]),
                   (caus, C["CAUS"]), (wlo, C["WLO"]), (idn, C["IDN"])]:
        k.load(sb, dr)
    kcs = k.sbuf("kcs", [128, NCP]); vcs = k.sbuf("vcs", [CW, NCC, 128])
    kss = k.sbuf("kss", [128, T]); vss = k.sbuf("vss", [128, NQ, 128])
    kws = k.sbuf("kws", [128, T]); vws = k.sbuf("vws", [128, NQ, 128])
    gts = k.sbuf("gts", [128, NQ, 12])
    qb = [k.sbuf(f"qb{i}", [128, 4, 128]) for i in range(2)]
    accb = [k.sbuf(f"acc{i}", [128, 4, 128]) for i in range(2)]
    srow = [k.sbuf(f"srow{i}", [128, T]) for i in range(2)]
    etb = [k.sbuf(f"et{i}", [128, NQ, 128]) for i in range(2)]
    scb = [k.sbuf(f"sc{i}", [128, NCP]) for i in range(2)]
    pcb = [k.sbuf(f"pc{i}", [128, NCP]) for i in range(2)]
    pcTb = [k.sbuf(f"pcT{i}", [CW, NCC, 128]) for i in range(2)]
    imp = k.sbuf("imp", [128, NBLK]); imp2 = k.sbuf("imp2", [128, NBLK]); seln = k.sbuf("seln", [128, NBLK])
    m8 = k.sbuf("m8", [128, 8]); m8b = k.sbuf("m8b", [128, 8])
    st_ = {n: [k.sbuf(f"st_{n}{i}", [128, 1]) for i in range(4)] for n in ["mx", "nmx", "sm", "ri"]}
    pS = [k.psum(f"pS{i}", [128, 512]) for i in range(2)]
    pT = [k.psum(f"pT{i}", [128, 512]) for i in range(2)]
    pO = [k.psum(f"pO{i}", [128, 512]) for i in range(2)]
    pI = k.psum("pI", [128, 512]); pC = k.psum("pC", [128, 512])
    ctr = {"s": 0, "t": 0, "o": 0, "st": 0, "row": 0, "et": 0}
    JC = lambda a: a.rearrange("p (j c) -> p j c", c=128)

    def nxt(name, n=2):
        v = ctr[name]
        ctr[name] = v + 1
        return v % n

    def softmax_pv(S, ntile, vsrc, kt0, gcol, acc_v):
        si = nxt("st", 4)
        mx, nmx, sm, ri = (st_[n][si] for n in ["mx", "nmx", "sm", "ri"])
        k.op("dve", nc.vector.tensor_reduce, [S], [mx[:]], out=mx[:].ap, in_=S.ap, axis=AX.X, op=ALU.max)
        k.ts(nmx[:], mx[:], -1.0, ALU.mult)
        k.act(S, S, AF.Exp, bias=nmx[:], accum_out=sm[:])
        et = etb[nxt("et")]
        for g0 in range(0, ntile, 4):
            n = min(4, ntile - g0)
            pt = pT[nxt("t")]
            for j in range(n):
                k.tr(pt[:, j * 128:(j + 1) * 128], S[:, (g0 + j) * 128:(g0 + j + 1) * 128], idn[:])
            k.copy(et[:, g0:g0 + n, :], pt[:, :n * 128].m(JC), eng=("act" if (g0 // 4) % 2 == 0 else "dve"))
        po = pO[nxt("o")]
        for j in range(ntile):
            k.mm(po[:, :128], et[:, j, :], vsrc[:, kt0 + j, :], start=(j == 0), stop=(j == ntile - 1))
        k.op("dve", nc.vector.reciprocal, [sm[:]], [ri[:]], out=ri[:].ap, in_=sm[:].ap)
        k.tt(ri[:], ri[:], gcol, ALU.mult)
        k.stt(acc_v, po[:, :128], ri[:], acc_v, ALU.mult, ALU.add)

    def tile_jobs(g, i):
        qt = qb[i % 2]
        acc = accb[i % 2]
        sel = i >= 8
        NCi = min(8 * i + 8, NCP)
        ncc = (NCi + CW - 1) // CW
        nk = i + 1
        kt0 = max(0, i - 4)
        nw = i - kt0 + 1
        jobs = []

        def cmp_A(r):
            def f():
                if r == 0:
                    k.dma("sp", qt[:], Y[g * 512:(g + 1) * 512, i * 128:(i + 1) * 128].m(
                        lambda a: a.rearrange("(r d) t -> d r t", d=128)), qt)
                sc = scb[r % 2]
                k.mm(pC[:, :NCi], qt[:, r, :], kcs[:, :NCi])
                lo = max(NCi - 16, 0)
                bw = NCi - lo
                if lo > 0:
                    k.copy(sc[:, :lo], pC[:, :lo])
                k.tt(sc[:, lo:NCi], pC[:, lo:NCi], band[:, 16 - bw:16], ALU.add)
            return f

        def cmp_B(r):
            def f():
                sc = scb[r % 2]
                pc = pcb[r % 2]
                pcT = pcTb[r % 2]
                si = nxt("st", 4)
                mx, nmx, sm, ri = (st_[n][si] for n in ["mx", "nmx", "sm", "ri"])
                k.op("dve", nc.vector.tensor_reduce, [sc[:, :NCi]], [mx[:]], out=mx[:].ap, in_=sc[:, :NCi].ap,
                     axis=AX.X, op=ALU.max)
                k.ts(nmx[:], mx[:], -1.0, ALU.mult)
                k.act(sc[:, :NCi], sc[:, :NCi], AF.Exp, bias=nmx[:], accum_out=sm[:])
                k.op("dve", nc.vector.reciprocal, [sm[:]], [ri[:]], out=ri[:].ap, in_=sm[:].ap)
                if i == 0:
                    k.tt(ri[:], ri[:], rv[:], ALU.mult)
                k.ts(pc[:, :NCi], sc[:, :NCi], ri[:], ALU.mult)
                pt = pT[nxt("t")]
                for cc in range(ncc):
                    w = min(CW, NCi - cc * CW)
                    k.tr(pt[:w, cc * 128:(cc + 1) * 128], pc[:, cc * CW:cc * CW + w], idn[:])
                for cc in range(ncc):
                    w = min(CW, NCi - cc * CW)
                    k.copy(pcT[:w, cc, :], pt[:w, cc * 128:(cc + 1) * 128])
                po = pO[nxt("o")]
                for cc in range(ncc):
                    w = min(CW, NCi - cc * CW)
                    k.mm(po[:, :128], pcT[:w, cc, :], vcs[:w, cc, :], start=(cc == 0), stop=(cc == ncc - 1))
                if sel:
                    for cc in range(ncc):
                        w = min(CW, NCi - cc * CW)
                        k.mm(pI[:, :NBLK], pcT[:w, cc, :], ovs[:w, cc, :], start=(r == 0 and cc == 0),
                             stop=(r == 3 and cc == ncc - 1))
                k.ts(acc.p(r)[:, r, :], po[:, :128], gts[:, i, r * 3:r * 3 + 1], ALU.mult)
            return f

        def sel_A(r):
            def f():
                if r == 0 and sel:
                    k.tt(imp[:], pI[:, :NBLK], keep[:, i, :], ALU.mult)
                    k.tt(imp[:], imp[:], addc[:, i, :], ALU.add)
                    k.op("dve", nc.vector.max, [imp[:]], [m8[:]], out=m8[:].ap, in_=imp[:].ap)
                    k.op("dve", nc.vector.match_replace, [m8[:], imp[:]], [imp2[:]], out=imp2[:].ap,
                         in_to_replace=m8[:].ap, in_values=imp[:].ap, imm_value=-3.0e38)
                    k.op("dve", nc.vector.max, [imp2[:]], [m8b[:]], out=m8b[:].ap, in_=imp2[:].ap)
                    k.ts(seln[:], imp[:], m8b[:, 7:8], ALU.is_ge)
                    k.ts(seln[:], seln[:], -1.0, ALU.add, 1.0e30, ALU.mult)
                S = srow[(2 * i + r) % 2]
                for kb in range(0, nk, 4):
                    ke = min(nk, kb + 4)
                    ps = pS[nxt("s")]
                    k.mm(ps[:, :(ke - kb) * 128], qt[:, r, :], kss[:, kb * 128:ke * 128])
                    has_diag = ke == nk
                    nfull = (ke - kb) - (1 if has_diag else 0)
                    if nfull > 0:
                        if sel:
                            k.tt(S[:, kb * 128:(kb + nfull) * 128].m(lambda a: a.rearrange("p (j c) -> p j c", c=64)),
                                 ps[:, :nfull * 128].m(lambda a: a.rearrange("p (j c) -> p j c", c=64)),
                                 seln[:, 2 * kb:2 * (kb + nfull)].m(lambda a, nf=nfull: a.unsqueeze(2).broadcast_to([128, 2 * nf, 64])),
                                 ALU.add)
                        else:
                            k.copy(S[:, kb * 128:(kb + nfull) * 128], ps[:, :nfull * 128])
                    if has_diag:
                        k.tt(S[:, i * 128:(i + 1) * 128], ps[:, nfull * 128:(nfull + 1) * 128], caus[:], ALU.add)
            return f

        def sel_B(r):
            def f():
                S = srow[(2 * i + r) % 2]
                softmax_pv(S[:, :nk * 128], nk, vss, 0, gts[:, i, r * 3 + 1:r * 3 + 2], acc.p(r)[:, r, :])
            return f

        def win_A(r):
            def f():
                S = srow[(2 * i + r) % 2]
                for g0 in range(0, nw, 4):
                    ge = min(nw, g0 + 4)
                    ps = pS[nxt("s")]
                    k.mm(ps[:, :(ge - g0) * 128], qt[:, r, :], kws[:, (kt0 + g0) * 128:(kt0 + ge) * 128])
                    for j in range(g0, ge):
                        kt = kt0 + j
                        dst = S[:, j * 128:(j + 1) * 128]
                        src = ps[:, (j - g0) * 128:(j - g0 + 1) * 128]
                        if kt == i:
                            k.tt(dst, src, caus[:], ALU.add)
                        elif kt == i - 4:
                            k.tt(dst, src, wlo[:], ALU.add)
                        else:
                            k.copy(dst, src)
            return f

        def win_B(r):
            def f():
                S = srow[(2 * i + r) % 2]
                softmax_pv(S[:, :nw * 128], nw, vws, kt0, gts[:, i, r * 3 + 2:r * 3 + 3], acc.p(r)[:, r, :])
                k.dma("pool", o_tok[i * 128:(i + 1) * 128, g * 512 + r * 128:g * 512 + (r + 1) * 128], acc.p(r)[:, r, :], acc)
            return f

        for r in range(4):
            jobs.append((False, cmp_A(r), cmp_B(r)))
        for r in range(4):
            jobs.append((r == 0 and sel, sel_A(r), sel_B(r)))
        for r in range(4):
            jobs.append((False, win_A(r), win_B(r)))
        return jobs

    for g in range(4):
        k.load(kcs, KCo[g]); k.load(vcs, VCo[g])
        k.load(kss, KV[(8 + g) * 128:(9 + g) * 128, :])
        k.load(kws, KV[(16 + g) * 128:(17 + g) * 128, :])
        for (vidx, vdst) in [(12 + g, vss), (20 + g, vws)]:
            k.load(srow[0], KV[vidx * 128:(vidx + 1) * 128, :])
            for i0 in range(0, NQ, 4):
                n = min(4, NQ - i0)
                pt = pT[nxt("t")]
                for j in range(n):
                    k.tr(pt[:, j * 128:(j + 1) * 128], srow[0][:, (i0 + j) * 128:(i0 + j + 1) * 128], idn[:])
                k.copy(vdst[:, i0:i0 + n, :], pt[:, :n * 128].m(JC), eng=("act" if (i0 // 4) % 2 == 0 else "dve"))
        k.dma("sp", srow[1][:12, :], G[g * 12:(g + 1) * 12, :], srow[1])
        for i in range(NQ):
            k.tr(pI[:, i * 12:(i + 1) * 12], srow[1][:12, i * 128:(i + 1) * 128], idn[:12, :12])
        k.copy(gts[:], pI[:, :NQ * 12].m(lambda a: a.rearrange("p (i c) -> p i c", c=12)))
        jobs = []
        for i in range(NQ):
            jobs += tile_jobs(g, i)
        pending = None
        for (barrier, fa, fb_) in jobs:
            if barrier and pending is not None:
                pending()
                pending = None
            fa()
            if pending is not None:
                pending()
            pending = fb_
        if pending is not None:
            pending()
    k.end_stage()


DEBUG = False


def build_fused(T):
    k = K()
    NC_ = T // 64
    NQ, NBLK, NCP = T // 128, T // 64, T // 16
    CW = min(128, NCP)
    NCC = NCP // CW
    ins = {}

    def ein(name, shape):
        ins[name] = k.dram(name, shape)
        return ins[name]
    xT = ein("xT", [D, T + 1])
    WA, WC = [], []
    for i in range(2):
        w = {"vec": ein(f"A{i}_vec", [128, len(VEC_A), 16]), "wq": ein(f"A{i}_wq", [4, 16, 128, 16, 128]),
             "w1": ein(f"A{i}_w1", [128, 16, 96]), "w2": ein(f"A{i}_w2", [96, D]),
             "a1": ein(f"A{i}_a1", [128, 16, 96]), "a2": ein(f"A{i}_a2", [96, D])}
        if i > 0:
            w["v1"] = ein(f"A{i}_v1", [128, 16, 64]); w["v2"] = ein(f"A{i}_v2", [64, D])
        WA.append(w)
        WC.append({"vec": ein(f"C{i}_vec", [128, 3, 16]), "wo": ein(f"C{i}_wo", [16, 128, 16, 128])})
    C = {}
    for name, shape in [("bd", [128, 128]), ("bdm", [128, 128]), ("ones", [128, 128]), ("rmask", [128, 128]),
                        ("IDN", [128, 128]), ("MN", [64, 8, 128]), ("ML", [64, 8, 64]), ("ID", [64, 8, 64]),
                        ("rt", [128, 128]), ("OV", [CW, NCC, NBLK]), ("BAND", [128, 16]), ("RV", [128, 1]),
                        ("KEEP", [128, NQ, NBLK]), ("ADDC", [128, NQ, NBLK]), ("CAUS", [128, 128]), ("WLO", [128, 128])]:
        C[name] = ein(name, shape)
    ct_kv, st_kv = ein("ct_kv", [128, T]), ein("st_kv", [128, T])
    ct_q, st_q = ein("ct_q", [128, T]), ein("st_q", [128, T])
    ct_c, st_c = ein("ct_c", [128, NCP]), ein("st_c", [128, NCP])
    kv_gv, kv_W = ein("kv_gv", [128, 16]), ein("kv_W", [24, 128, 16, 128])
    PE, W1, W2 = ein("PE", [2, 128, 32]), ein("W1", [2, 128, 32, 256]), ein("W2", [2, 128, 2, 128])
    WN = []
    for j in range(2):
        WN.append({"gv": ein(f"N{j}_gv", [128, 16]), "W": ein(f"N{j}_W", [32, 128, 16, 128]),
                   "wg": ein(f"N{j}_wg", [128, 16, 48]), "gb": ein(f"N{j}_gb", [48, 1]),
                   "vec": ein(f"N{j}_vec", [128, 3, 16]), "wo": ein(f"N{j}_wo", [16, 128, 16, 128])})
    outT = k.dram("outT", [D, T], kind="ExternalOutput")
    itn = lambda name, shape: k.dram(name, shape, kind=("ExternalOutput" if DEBUG else "Internal"))
    hbufs = [k.dram(f"hbuf{i}", [D, T + 1], kind=("ExternalOutput" if DEBUG else "Internal")) for i in range(3)]
    S = {n: itn("s_" + n, [D, T]) for n in A_OUTS}
    vf0 = itn("vf0", [D, T])
    pc = itn("pc", [D, NC_])
    o_tok = itn("o_tok", [T, D])
    KV = itn("KV", [3072, T]); Y = itn("Y", [4096, T]); G = itn("G", [48, T])
    KCo = itn("KCo", [4, 128, NCP]); VCo = itn("VCo", [4, CW, NCC, 128])

    k.begin_stage("Z")
    zt = k.sbuf("zt", [128, 16, 1])
    k.memset(zt[:], 0.0)
    for hb in hbufs:
        k.dma("sp", hb[:, 0:1].m(FM), zt[:], zt, nonc=True)
    k.end_stage()

    kinds_kv = ["rope" if (c // 4) in (2, 4) else "copy" for c in range(24)]
    kinds_in = ["rope"] * 16 + ["silu"] * 16
    hseq = [xT] + hbufs
    hidx = 0
    hin, hout = hseq[0], hseq[1]
    for i in range(2):
        outs = dict(S)
        if i == 0:
            outs["vv"] = vf0
        stage_rwkv_pre(k, T, i > 0, hin, WA[i], C, outs, pc, vf0)
        stage_scan(k, T, outs, pc, C, o_tok)
        if DEBUG == "B0":
            return k.finish()
        stage_post(k, T, "rwkv", o_tok, S["sz"], (hin, 1), S["bonus"], WC[i]["vec"], WC[i]["wo"], C, hout, 1, False)
        hidx += 1
        hin, hout = hseq[hidx], hseq[min(hidx + 1, 3)]
    stage_proj(k, T, kinds_kv, (hin, 1), kv_gv, kv_W, None, None, ct_kv, st_kv, C, KV, G)
    stage_cmp(k, T, KV, PE, W1, W2, ct_c, st_c, C, KCo, VCo)
    for j in range(2):
        stage_proj(k, T, kinds_in, (hin, 1), WN[j]["gv"], WN[j]["W"], WN[j]["wg"], WN[j]["gb"], ct_q, st_q, C, Y, G)
        stage_attn(k, T, Y, G, KV, KCo, VCo, C, o_tok)
        fin = j == 1
        stage_post(k, T, "nsa_final" if fin else "nsa", o_tok, Y[2048:4096, :], (hin, 1), None, WN[j]["vec"], WN[j]["wo"], C,
                   outT if fin else hout, 0 if fin else 1, fin)
        hidx += 1
        hin, hout = hseq[min(hidx, 3)], hseq[min(hidx + 1, 3)]
    return k.finish()


def vec16(v):
    return np.ascontiguousarray(np.asarray(v, np.float32).reshape(16, 128).T)


def wlay(W):
    n_out = W.shape[1]
    oc = n_out // 128
    return np.ascontiguousarray(W.reshape(16, 128, oc, 128).transpose(2, 1, 0, 3))


def w1lay(W):
    return np.ascontiguousarray(W.reshape(16, 128, W.shape[1]).transpose(1, 0, 2))


def const_bd():
    m = np.zeros((128, 128), np.float32)
    m[:64, :64] = 1.0
    m[64:, 64:] = 1.0
    return m


def scan_consts():
    s = np.arange(64)[:, None]
    t = np.arange(64)[None, :]
    su = (s < t).astype(np.float32)
    ui = (s <= t).astype(np.float32)
    mn = np.tile(np.concatenate([su, ui], axis=1)[:, None, :], (1, 8, 1))
    ml = np.tile((t < s).astype(np.float32)[:, None, :], (1, 8, 1))
    idt = np.tile(np.eye(64, dtype=np.float32)[:, None, :], (1, 8, 1))
    return np.ascontiguousarray(mn), np.ascontiguousarray(ml), np.ascontiguousarray(idt)


def rope_consts(pos, scale=1.0):
    half = 16
    inv = (500000.0 ** (-np.arange(half, dtype=np.float32) / half)).astype(np.float32)
    ang = pos.astype(np.float32)[None, :] * inv[:, None]
    ct = np.ones((128, len(pos)), np.float32)
    st = np.zeros((128, len(pos)), np.float32)
    ct[:16] = np.cos(ang); ct[16:32] = np.cos(ang)
    st[:16] = np.sin(ang); st[16:32] = np.sin(ang)
    rt = np.zeros((128, 128), np.float32)
    for i in range(16):
        rt[i + 16, i] = -1.0
        rt[i, i + 16] = 1.0
    return (ct * scale).astype(np.float32), (st * scale).astype(np.float32), rt


def attn_consts(T):
    NQ, NBLK, NCP = T // 128, T // 64, T // 16
    NCM = NCP - 1
    CW = min(128, NCP)
    NCC = NCP // CW
    ov = np.zeros((NCP, NBLK), np.float32)
    for c in range(NCM):
        for l in range(32):
            ov[c, (16 * c + l) // 64] += 1.0 / 32.0
    OV = np.ascontiguousarray(ov.reshape(NCC, CW, NBLK).transpose(1, 0, 2))
    t = np.arange(128)[:, None]
    j = np.arange(16)[None, :]
    band = np.where(t >= 16 * j - 97, 0.0, NEG).astype(np.float32)
    rv = (np.arange(128) >= 31).astype(np.float32).reshape(128, 1)
    tq = (np.arange(NQ)[None, :] * 128 + np.arange(128)[:, None])[:, :, None]
    cur = tq // 64
    blk = np.arange(NBLK)[None, None, :]
    forced = (blk == 0) | (blk == cur) | (blk == cur - 1)
    future = blk > cur
    keep = np.where(forced | future, 0.0, 1.0).astype(np.float32)
    addc = np.where(forced, 1.0e4, np.where(future, NEG, 0.0)).astype(np.float32)
    r = np.arange(128)[:, None]
    c = np.arange(128)[None, :]
    caus = np.where(c <= r, 0.0, NEG).astype(np.float32)
    wlo = np.where(c > r, 0.0, NEG).astype(np.float32)
    return {"OV": OV, "BAND": band, "RV": rv, "KEEP": np.ascontiguousarray(keep), "ADDC": np.ascontiguousarray(addc),
            "CAUS": caus, "WLO": wlo, "IDN": np.eye(128, dtype=np.float32)}


_PROGS = {}


def forward(x, P):
    B, T, _ = x.shape
    if T not in _PROGS:
        _PROGS[T] = build_fused(T)
    nc = _PROGS[T]
    NCP = T // 16
    z = np.zeros(D, np.float32)
    com = {}
    for i in range(2):
        vres = i > 0
        vecs = {"ng": P["a_norm_g"][i], "w0": P["a_w0"][i], "a0": P["a_a0"][i],
                "v0": P["a_v0"][i - 1] if vres else z,
                "k_k": P["a_k_k"][i], "k_a": P["a_k_a"][i], "r_k": P["a_r_k"][i].reshape(D)}
        for j in range(6):
            vecs[f"mu{j}"] = P["a_mu"][i, j]
        com[f"A{i}_vec"] = np.ascontiguousarray(np.stack([vec16(vecs[n]) for n in VEC_A], axis=1))
        com[f"A{i}_wq"] = np.stack([wlay(P["a_w_rkvz"][i, j]) for j in range(4)])
        com[f"A{i}_w1"] = w1lay(P["a_w1"][i]); com[f"A{i}_w2"] = np.ascontiguousarray(P["a_w2"][i])
        com[f"A{i}_a1"] = w1lay(P["a_a1"][i]); com[f"A{i}_a2"] = np.ascontiguousarray(P["a_a2"][i])
        if vres:
            com[f"A{i}_v1"] = w1lay(P["a_v1"][i - 1]); com[f"A{i}_v2"] = np.ascontiguousarray(P["a_v2"][i - 1])
        com[f"C{i}_vec"] = np.ascontiguousarray(np.stack([vec16(v) for v in (P["a_ln_g"][i], P["a_ln_b"][i], z)], axis=1))
        com[f"C{i}_wo"] = wlay(P["a_w_o"][i])
    rmask = np.ones((128, 128), np.float32)
    rmask[:, 0::64] = 0.0
    mn, ml, idt = scan_consts()
    com.update({"bd": const_bd(), "bdm": (const_bd() / 64.0).astype(np.float32), "ones": np.ones((128, 128), np.float32),
                "rmask": rmask, "MN": mn, "ML": ml, "ID": idt})
    com.update(attn_consts(T))
    pos = np.arange(T)
    com["ct_kv"], com["st_kv"], com["rt"] = rope_consts(pos)
    com["ct_q"], com["st_q"], _ = rope_consts(pos, 128.0 ** -0.5)
    com["ct_c"], com["st_c"], _ = rope_consts(np.arange(NCP) * 16 + 31)
    com["kv_gv"] = vec16(P["kv_norm_g"]); com["kv_W"] = wlay(P["kv_w"])
    com["PE"] = np.ascontiguousarray(P["cmp_pe"].transpose(0, 2, 1))
    com["W1"] = np.ascontiguousarray(P["cmp_w1"].reshape(2, 32, 128, 256).transpose(0, 2, 1, 3))
    com["W2"] = np.ascontiguousarray(P["cmp_w2"].reshape(2, 2, 128, 128).transpose(0, 2, 1, 3))
    for j in range(2):
        w_in = P["b_w_in"][j]
        com[f"N{j}_gv"] = vec16(P["b_norm_g"][j])
        com[f"N{j}_W"] = wlay(np.ascontiguousarray(np.concatenate([w_in[:, :2048], w_in[:, 2096:]], axis=1)))
        com[f"N{j}_wg"] = w1lay(np.ascontiguousarray(w_in[:, 2048:2096]))
        com[f"N{j}_gb"] = np.ascontiguousarray(P["b_gate_b"][j].reshape(48, 1))
        com[f"N{j}_vec"] = np.ascontiguousarray(np.stack([vec16(v) for v in (z, z, P["final_g"])], axis=1))
        com[f"N{j}_wo"] = wlay(P["b_w_o"][j])
    in_maps = []
    for c in range(NCORES):
        b = c % B
        m = dict(com)
        m["xT"] = np.ascontiguousarray(np.concatenate([np.zeros((1, D), np.float32), x[b]], axis=0).T)
        in_maps.append(m)
    res = run_bass_kernel_spmd(nc, in_maps, core_ids=list(range(NCORES)))
    forward.res = res
    if DEBUG == "B0":
        return None
    if DEBUG:
        forward.dbg = [[res.results[b][f"hbuf{i}"][:, 1:].T for b in range(B)] for i in range(3)]
    out = np.stack([res.results[b]["outT"].T for b in range(B)])
    return np.ascontiguousarray(out)


def kernel(**inputs):
    P = {k_: np.asarray(v, dtype=np.float32) for k_, v in inputs.items()}
    return forward(P["x"], P).astype(np.float32)
```

```python
import numpy as np
from contextlib import ExitStack
import concourse.bass as bass
import concourse.mybir as mybir
from concourse.bass_utils import run_bass_kernel_spmd

F32 = mybir.dt.float32
AF = mybir.ActivationFunctionType
ALU = mybir.AluOpType
AX = mybir.AxisListType

D = 2048
NCH = 16
NCORES = 8
HS = 64
NEG = -1.0e30


class Trk:
    __slots__ = ("w", "r", "name")

    def __init__(self, name):
        self.w = None
        self.r = {}
        self.name = name


class V:
    __slots__ = ("ap", "trk")

    def __init__(self, ap, trk):
        self.ap = ap
        self.trk = trk

    def m(self, fn):
        return V(fn(self.ap), self.trk)

    def __getitem__(self, idx):
        return V(self.ap[idx], self.trk)


class Buf:
    def __init__(self, t, name):
        self.t = t
        self.name = name
        self.trk = Trk(name)
        self.parts = {}
        self.dkey = None

    def __getitem__(self, idx):
        return V(self.t[idx], self.trk)

    def p(self, key):
        tr = self.parts.get(key)
        if tr is None:
            tr = Trk(f"{self.name}:{key}")
            self.parts[key] = tr
        return _PartView(self, tr)


class _PartView:
    def __init__(self, buf, trk):
        self.buf = buf
        self.trk = trk

    def __getitem__(self, idx):
        return V(self.buf.t[idx], self.trk)


class K:
    def __init__(self):
        self.nc = bass.Bass("TRN2", target_bir_lowering=False)
        self.es = ExitStack()
        nc = self.nc
        self.eng = {"pe": nc.tensor, "act": nc.scalar, "dve": nc.vector,
                    "pool": nc.gpsimd, "sp": nc.sync}
        self.sem = {}
        self.cnt = {}
        self.waited = {}
        self.semobj = {}
        for e in self.eng:
            self.sem[e] = self.es.enter_context(nc.semaphore("s_" + e))
            self.cnt[e] = 0
            self.waited[e] = {}
            self.semobj[("e", e)] = self.sem[e]
        self.tag = "g"
        self.out_events = []
        self.dcur = {}
        self.free_dsems = []
        self.ndsem = 0
        self.stage_es = None
        self.stage_bufs = []

    def dram(self, name, shape, kind="ExternalInput"):
        t = self.nc.dram_tensor(name, list(shape), F32, kind=kind)
        return Buf(t.ap(), name)

    def begin_stage(self, tag):
        self.stage_es = ExitStack()
        self.stage_bufs = []
        self.stage_no = getattr(self, "stage_no", 0) + 1
        self.tag = f"{tag}{self.stage_no}"

    def end_stage(self):
        need = {("e", e): self.cnt[e] for e in self.eng if self.cnt[e] > 0}
        for key, val in self.dcur.items():
            if val > 0:
                need[key] = val
        for e in self.eng:
            self._emit_waits(e, dict(need))
        for b in self.stage_bufs:
            if b.dkey is not None:
                self.free_dsems.append(b.dkey)
        self.stage_es.close()
        self.stage_es = None

    def sbuf(self, name, shape):
        st = self.stage_es if self.stage_es is not None else self.es
        t = st.enter_context(self.nc.sbuf_tensor(f"{self.tag}_{name}", list(shape), F32))
        b = Buf(t, f"{self.tag}_{name}")
        self.stage_bufs.append(b)
        return b

    def psum(self, name, shape):
        st = self.stage_es if self.stage_es is not None else self.es
        t = st.enter_context(self.nc.psum_tensor(f"{self.tag}_{name}", list(shape), F32))
        return Buf(t, f"{self.tag}_{name}")

    def _need(self, eng, reads, writes):
        need = {}

        def add(key, val):
            if key == ("e", "pe") and eng == "pe":
                return
            if key[0] == "d":
                val = self.dcur[key]
            if need.get(key, 0) < val:
                need[key] = val
        for t in reads:
            if t.w is not None:
                add(*t.w)
        for t in writes:
            if t.w is not None:
                add(*t.w)
            for key, val in t.r.items():
                add(key, val)
        return need

    def _emit_waits(self, eng, need):
        w = self.waited[eng]
        for key, val in need.items():
            if w.get(key, 0) >= val:
                continue
            self.eng[eng].wait_ge(self.semobj[key], val)
            w[key] = val

    def _mark(self, ev, reads, writes):
        key, val = ev
        for t in writes:
            t.w = ev
            t.r = {}
        for t in reads:
            if t.r.get(key, 0) < val:
                t.r[key] = val

    def op(self, eng, fn, reads, writes, *a, **kw):
        reads = [v.trk for v in reads if isinstance(v, V)]
        writes = [v.trk for v in writes if isinstance(v, V)]
        self._emit_waits(eng, self._need(eng, reads, writes))
        inst = fn(*a, **kw)
        self.cnt[eng] += 1
        inst.then_inc(self.sem[eng], 1)
        self._mark((("e", eng), self.cnt[eng]), reads, writes)
        return inst

    def dma(self, q, out, in_, owner, is_output=False, nonc=False):
        reads = [in_.trk]
        writes = [out.trk]
        self._emit_waits(q, self._need(q, reads, writes))
        if owner.dkey is None:
            if self.free_dsems:
                owner.dkey = self.free_dsems.pop()
            else:
                key = ("d", self.ndsem)
                self.ndsem += 1
                self.semobj[key] = self.es.enter_context(self.nc.semaphore(f"dq{key[1]}"))
                self.dcur[key] = 0
                owner.dkey = key
        key = owner.dkey
        if nonc:
            with self.nc.allow_non_contiguous_dma(reason="tiny strided column transfer"):
                inst = self.eng[q].dma_start(out=out.ap, in_=in_.ap)
        else:
            inst = self.eng[q].dma_start(out=out.ap, in_=in_.ap)
        self.dcur[key] += 16
        inst.then_inc(self.semobj[key], 16)
        ev = (key, self.dcur[key])
        self._mark(ev, reads, writes)
        if is_output:
            self.out_events.append(ev)

    def load(self, sb, dr, q="sp"):
        self.dma(q, sb[:] if isinstance(sb, Buf) else sb, dr[:] if isinstance(dr, Buf) else dr,
                 sb if isinstance(sb, Buf) else None)

    @staticmethod
    def _a(x):
        return x.ap if isinstance(x, V) else x

    def tt(self, out, in0, in1, op, eng="dve"):
        e = self.eng[eng]
        return self.op(eng, e.tensor_tensor, [in0, in1], [out], out=out.ap, in0=in0.ap, in1=in1.ap, op=op)

    def ts(self, out, in0, s1, op0, s2=None, op1=None, eng="dve", accum_out=None):
        e = self.eng[eng]
        kw = dict(out=out.ap, in0=in0.ap, scalar1=self._a(s1), scalar2=self._a(s2), op0=op0)
        if op1 is not None:
            kw["op1"] = op1
        w = [out]
        if accum_out is not None:
            kw["accum_out"] = accum_out.ap
            w.append(accum_out)
        return self.op(eng, e.tensor_scalar, [in0, s1, s2], w, **kw)

    def stt(self, out, in0, s, in1, op0, op1):
        return self.op("dve", self.nc.vector.scalar_tensor_tensor, [in0, s, in1], [out],
                       out=out.ap, in0=in0.ap, scalar=self._a(s), in1=in1.ap, op0=op0, op1=op1)

    def act(self, out, in_, func, bias=None, scale=None, accum_out=None):
        kw = dict(out=out.ap, in_=in_.ap, func=func)
        if bias is not None:
            kw["bias"] = self._a(bias)
        if scale is not None:
            kw["scale"] = self._a(scale)
        w = [out]
        if accum_out is not None:
            kw["accum_out"] = accum_out.ap
            w.append(accum_out)
        return self.op("act", self.nc.scalar.activation, [in_, bias, scale], w, **kw)

    def copy(self, out, in_, eng="act"):
        if eng == "act":
            return self.op("act", self.nc.scalar.copy, [in_], [out], out=out.ap, in_=in_.ap)
        return self.op(eng, self.eng[eng].tensor_copy, [in_], [out], out=out.ap, in_=in_.ap)

    def mm(self, out, lhsT, rhs, start=True, stop=True):
        return self.op("pe", self.nc.tensor.matmul, [lhsT, rhs], [out], out.ap, lhsT.ap, rhs.ap,
                       start=start, stop=stop)

    def tr(self, out, in_, ident):
        return self.op("pe", self.nc.tensor.transpose, [in_, ident], [out], out.ap, in_.ap, ident.ap)

    def memset(self, out, val, eng="pool"):
        return self.op(eng, self.eng[eng].memset, [], [out], out.ap, val)

    def finish(self):
        need = {}
        for key, val in self.out_events:
            if need.get(key, 0) < val:
                need[key] = val
        for e in self.eng:
            if e != "sp" and self.cnt[e] > 0:
                need[("e", e)] = self.cnt[e]
        self._emit_waits("sp", need)
        self.es.close()
        return self.nc


FM = lambda a: a.rearrange("(c p) t -> p c t", p=128)


def rmsnorm_fm(k, x, u, gvec, ones, ps, sq, rstd, ncols):
    for c in range(NCH):
        k.act(sq.p(c)[:, c, :ncols], x[:, c, :ncols], AF.Square)
    for c in range(NCH):
        k.mm(ps[:, :ncols], ones[:], sq.p(c)[:, c, :ncols], start=(c == 0), stop=(c == NCH - 1))
    k.ts(rstd[:, :ncols], ps[:, :ncols], 1.0 / D, ALU.mult, 1e-6, ALU.add)
    k.act(rstd[:, :ncols], rstd[:, :ncols], AF.Ln)
    k.act(rstd[:, :ncols], rstd[:, :ncols], AF.Exp, scale=-0.5)
    for c in range(NCH):
        k.stt(u.p(c)[:, c, :ncols], x[:, c, :ncols], gvec[:, c:c + 1], rstd[:, :ncols], ALU.mult, ALU.mult)


A_OUTS = ["atil", "rtil", "btil", "ktil", "vv", "sz", "bonus"]
VEC_A = ["ng", "mu0", "mu1", "mu2", "mu3", "mu4", "mu5", "w0", "a0", "v0", "k_k", "k_a", "r_k"]


def stage_rwkv_pre(k, T, vres, hT, W, C, outs, pc, vf):
    k.begin_stage("A")
    TT = 128
    NT = T // TT
    vecs = k.sbuf("vecs", [128, len(VEC_A), 16])
    negw0 = k.sbuf("negw0", [128, 16])
    w1s = k.sbuf("w1s", [128, 16, 96]); w2s = k.sbuf("w2s", [96, D])
    a1s = k.sbuf("a1s", [128, 16, 96]); a2s = k.sbuf("a2s", [96, D])
    v1s = k.sbuf("v1s", [128, 16, 64]); v2s = k.sbuf("v2s", [64, D])
    bds = k.sbuf("bds", [128, 128]); ones = k.sbuf("oness", [128, 128]); rms = k.sbuf("rms", [128, TT])
    lst = [(vecs, W["vec"]), (w1s, W["w1"]), (w2s, W["w2"]), (a1s, W["a1"]), (a2s, W["a2"]),
           (bds, C["bd"]), (ones, C["ones"]), (rms, C["rmask"])]
    if vres:
        lst += [(v1s, W["v1"]), (v2s, W["v2"])]
    for sb, dr in lst:
        k.load(sb, dr)
    wq = W["wq"]
    vi = {n: i for i, n in enumerate(VEC_A)}
    vcol = lambda n, c: vecs[:, vi[n], c:c + 1]
    k.ts(negw0[:], vecs[:, vi["w0"], :], -1.0, ALU.mult)

    ht = k.sbuf("ht", [128, 16, TT + 1])
    sq = k.sbuf("sq", [128, 16, TT + 1])
    u = k.sbuf("u", [128, 16, TT + 1])
    xx = sq
    rstd = k.sbuf("rstd", [128, TT + 1])
    mix = k.sbuf("mix", [128, 4, 16, TT])
    wb = [k.sbuf(f"wb{i}", [128, 16, 128]) for i in range(6)]
    h1w = k.sbuf("h1w", [96, TT]); h1a = k.sbuf("h1a", [96, TT]); h1v = k.sbuf("h1v", [64, TT])
    ps_n = k.psum("ps_n", [128, 512])
    ps_p = [k.psum(f"ps_p{i}", [128, 512]) for i in range(4)]
    ps_l = k.psum("ps_l", [128, 512])
    ps_m = k.psum("ps_m", [128, 512])
    ps_h = k.psum("ps_h", [128, 512])
    E = {}
    for n in ["r", "kx", "v", "sz", "e", "a", "sv", "kkr", "t1", "t2", "km", "cum", "p", "pinv", "pprev",
              "o_a", "o_r", "o_b", "o_k", "o_bn", "vft"]:
        nb = 2 if (n.startswith("o_") or n in ("sz", "v", "p")) else 1
        E[n] = [k.sbuf(f"e_{n}{i}", [128, TT]) for i in range(nb)]

    wcount = 0
    for t in range(NT):
        t0 = t * TT
        k.load(ht, hT[:, t0:t0 + TT + 1].m(FM))
        rmsnorm_fm(k, ht, u, vecs[:, vi["ng"], :], ones, ps_n, sq, rstd, TT + 1)
        for c in range(NCH):
            k.tt(xx.p(c)[:, c, 0:TT], u.p(c)[:, c, 0:TT], u.p(c)[:, c, 1:TT + 1], ALU.subtract)
        for i in range(4):
            for c in range(NCH):
                k.stt(mix.p((i, c))[:, i, c, :], xx.p(c)[:, c, 0:TT], vcol(f"mu{i}", c),
                      u.p(c)[:, c, 1:TT + 1], ALU.mult, ALU.add)
        for (ws, mi, dst, nr, fn) in [(w1s, 4, h1w, 96, AF.Tanh), (a1s, 5, h1a, 96, None), (v1s, 2, h1v, 64, None)]:
            if ws is v1s and not vres:
                continue
            if mi >= 4:
                for c in range(NCH):
                    k.stt(ht[:, c, 0:TT], xx.p(c)[:, c, 0:TT], vcol(f"mu{mi}", c),
                          u.p(c)[:, c, 1:TT + 1], ALU.mult, ALU.add)
            for c in range(NCH):
                src = ht[:, c, 0:TT] if mi >= 4 else mix.p((mi, c))[:, mi, c, :]
                k.mm(ps_h[:nr, :TT], ws[:, c, :], src, start=(c == 0), stop=(c == NCH - 1))
            if fn is None:
                k.copy(dst[:nr, :], ps_h[:nr, :TT])
            else:
                k.act(dst[:nr, :], ps_h[:nr, :TT], fn)
        for oc in range(NCH):
            w = []
            for pj in range(4):
                wbuf = wb[wcount % 6]
                wcount += 1
                k.dma("sp", wbuf[:], wq[pj, oc, :, :, :], wbuf)
                w.append(wbuf)
            e = {n: E[n][oc % len(E[n])] for n in E}
            osl = slice(oc * 128, (oc + 1) * 128)
            if vres:
                k.dma("sp", e["vft"][:], vf[osl, t0:t0 + TT], e["vft"])
            pcol = (oc % 2) * 128
            PP = [ps_p[pj].p(oc % 2)[:, pcol:pcol + TT] for pj in range(4)]
            for pj in range(4):
                for c in range(NCH):
                    k.mm(PP[pj], w[pj][:, c, :], mix.p((pj, c))[:, pj, c, :],
                         start=(c == 0), stop=(c == NCH - 1))
            k.mm(ps_l[:, 0:TT], w2s[:, osl], h1w[:, :])
            k.mm(ps_l[:, 128:128 + TT], a2s[:, osl], h1a[:, :])
            if vres:
                k.mm(ps_l[:, 256:256 + TT], v2s[:, osl], h1v[:, :])
            r, kx, v, a = e["r"], e["kx"], e["v"], e["a"]
            k.copy(r[:], PP[0])
            k.copy(kx[:], PP[1])
            k.copy(v[:], PP[2], eng="dve")
            k.act(e["sz"][:], PP[3], AF.Silu)
            k.dma("pool", outs["sz"][osl, t0:t0 + TT], e["sz"][:], e["sz"])
            if vres:
                k.act(e["sv"][:], ps_l[:, 256:256 + TT], AF.Sigmoid, bias=vcol("v0", oc))
                k.tt(e["t1"][:], e["vft"][:], v[:], ALU.subtract)
                k.tt(e["t1"][:], e["t1"][:], e["sv"][:], ALU.mult)
                k.tt(v[:], v[:], e["t1"][:], ALU.add)
            k.dma("pool", outs["vv"][osl, t0:t0 + TT], v[:], v)
            k.act(e["t2"][:], ps_l[:, 0:TT], AF.Exp, bias=negw0[:, oc:oc + 1], scale=-1.0)
            k.act(e["t2"][:], e["t2"][:], AF.Ln, bias=1.0)
            k.act(e["e"][:], e["t2"][:], AF.Exp, bias=-0.5, scale=-1.0)
            k.act(a[:], ps_l[:, 128:128 + TT], AF.Sigmoid, bias=vcol("a0", oc))
            k.ts(e["kkr"][:], kx[:], vcol("k_k", oc), ALU.mult)
            k.tt(e["t1"][:], e["kkr"][:], e["kkr"][:], ALU.mult)
            k.mm(ps_m[:, 0:TT], bds[:], e["t1"][:])
            k.ts(e["t1"][:], ps_m[:, 0:TT], 1e-24, ALU.max)
            k.act(e["t1"][:], e["t1"][:], AF.Ln)
            k.act(e["t1"][:], e["t1"][:], AF.Exp, scale=-0.5)
            k.tt(e["kkr"][:], e["kkr"][:], e["t1"][:], ALU.mult)
            k.ts(e["t1"][:], a[:], -1.0, ALU.add, vcol("k_a", oc), ALU.mult)
            k.stt(e["km"][:], e["t1"][:], 1.0, kx[:], ALU.add, ALU.mult)
            k.stt(e["t1"][:], r[:], vcol("r_k", oc), e["km"][:], ALU.mult, ALU.mult)
            k.mm(ps_m[:, 128:128 + TT], bds[:], e["t1"][:])
            k.tt(e["o_bn"][:], ps_m[:, 128:128 + TT], v[:], ALU.mult)
            k.dma("pool", outs["bonus"][osl, t0:t0 + TT], e["o_bn"][:], e["o_bn"])
            k.op("dve", k.nc.vector.tensor_tensor_scan, [rms[:], e["e"][:]], [e["cum"][:]],
                 out=e["cum"][:].ap, data0=rms[:].ap, data1=e["e"][:].ap, initial=0.0,
                 op0=ALU.mult, op1=ALU.add)
            k.act(e["p"][:], e["cum"][:], AF.Exp, scale=-1.0)
            k.act(e["pinv"][:], e["cum"][:], AF.Exp)
            k.tt(e["t1"][:], e["cum"][:], e["e"][:], ALU.subtract)
            k.act(e["pprev"][:], e["t1"][:], AF.Exp, scale=-1.0)
            k.dma("pool", pc[osl, 2 * t:2 * t + 2], e["p"][:, 63:TT:64], e["p"], nonc=True)
            k.stt(e["o_a"][:], e["kkr"][:], -1.0, e["pprev"][:], ALU.mult, ALU.mult)
            k.dma("pool", outs["atil"][osl, t0:t0 + TT], e["o_a"][:], e["o_a"])
            k.tt(e["o_r"][:], r[:], e["p"][:], ALU.mult)
            k.dma("pool", outs["rtil"][osl, t0:t0 + TT], e["o_r"][:], e["o_r"])
            k.tt(e["t1"][:], e["kkr"][:], a[:], ALU.mult)
            k.tt(e["o_b"][:], e["t1"][:], e["pinv"][:], ALU.mult)
            k.dma("pool", outs["btil"][osl, t0:t0 + TT], e["o_b"][:], e["o_b"])
            k.tt(e["o_k"][:], e["km"][:], e["pinv"][:], ALU.mult)
            k.dma("pool", outs["ktil"][osl, t0:t0 + TT], e["o_k"][:], e["o_k"])
    k.end_stage()


def stage_scan(k, T, A, pc, C, o_tok):
    k.begin_stage("B")
    NC_ = T // 64
    NH = 16
    GH = 8
    pcs = k.sbuf("pcs", [64, NH, NC_]); mn = k.sbuf("mn", [64, GH, 128])
    ml = k.sbuf("ml", [64, GH, 64]); idt = k.sbuf("idt", [64, GH, 64])
    for sb, dr in [(mn, C["MN"]), (ml, C["ML"]), (idt, C["ID"])]:
        k.load(sb, dr)
    H = k.sbuf("H", [64, NH, 64])
    NBUF = 2
    ar = [k.sbuf(f"ar{i}", [64, NH, 2, 64]) for i in range(NBUF)]
    fb = [k.sbuf(f"fb{i}", [64, NH, 64]) for i in range(NBUF)]
    fk = [k.sbuf(f"fk{i}", [64, NH, 64]) for i in range(NBUF)]
    fv = [k.sbuf(f"fv{i}", [64, NH, 64]) for i in range(NBUF)]
    tb = [k.sbuf(f"tb{i}", [64, NH, 64]) for i in range(NBUF)]
    tk = [k.sbuf(f"tk{i}", [64, NH, 64]) for i in range(NBUF)]
    tv = [k.sbuf(f"tv{i}", [64, NH, 64]) for i in range(NBUF)]
    osb = [k.sbuf(f"osb{i}", [64, NH, 64]) for i in range(2)]
    nbq = [[k.sbuf(f"nb{g}_{i}", [64, GH, 128]) for i in range(2)] for g in range(2)]
    nkq = [[k.sbuf(f"nk{g}_{i}", [64, GH, 128]) for i in range(2)] for g in range(2)]
    Pfin = [[k.sbuf(f"Pf{g}_{i}", [64, GH, 64]) for i in range(2)] for g in range(2)]
    Ls = [[k.sbuf(f"Ls{g}_{i}", [64, GH, 64]) for i in range(2)] for g in range(2)]
    Ns = [[k.sbuf(f"Ns{g}_{i}", [64, GH, 64]) for i in range(2)] for g in range(2)]
    Pm = [[k.sbuf(f"Pm{g}_{i}", [64, GH, 64]) for i in range(2)] for g in range(2)]
    xs = [k.sbuf(f"xs{g}", [64, GH, 64]) for g in range(2)]
    us = [k.sbuf(f"us{g}", [64, GH, 64]) for g in range(2)]
    htmp = [k.sbuf(f"htmp{g}", [64, GH, 64]) for g in range(2)]
    pb = [k.psum(f"pb{i}", [128, 512]) for i in range(8)]
    J8 = lambda a: a.rearrange("p (j c) -> p j c", j=GH)
    J4 = lambda a: a.rearrange("p (j c) -> p j c", j=4)
    it = 0
    tcnt = 0
    for hf in range(2):
        F0 = hf * 1024
        fm = lambda name, n: A[name][F0:F0 + 1024, n * 64:(n + 1) * 64].m(lambda a: a.rearrange("(h k) t -> k h t", k=64))
        k.load(pcs, pc[F0:F0 + 1024, :].m(lambda a: a.rearrange("(h k) n -> k h n", k=64)))
        for g in range(NH // GH):
            k.memset(H.p(g)[:, g * GH:(g + 1) * GH, :], 0.0)
        grp = list(range(NH // GH))
        R_ = lambda a: a.rearrange("p a t -> p (a t)")
        LB = {0: (pb[5], pb[6], pb[7]), 1: (pb[0], pb[1], pb[2])}
        BS = {0: pb[3], 1: pb[4]}

        def par_begin(n):
            nonlocal tcnt
            bi = n % NBUF
            k.dma("sp", ar[bi].p(0)[:, :, 0, :], fm("atil", n), ar[bi])
            k.dma("sp", ar[bi].p(0)[:, :, 1, :], fm("rtil", n), ar[bi])
            k.load(fb[bi], fm("btil", n))
            k.load(fk[bi], fm("ktil", n))
            k.load(fv[bi], fm("vv", n))
            a_ = ar[bi].p(0)
            for (src, dst) in [(fb[bi], tb[bi]), (fk[bi], tk[bi]), (fv[bi], tv[bi])]:
                for g in grp:
                    bank = pb[5 + tcnt % 3]
                    tcnt += 1
                    for j in range(GH):
                        k.tr(bank[:64, j * 64:(j + 1) * 64], src[:, g * GH + j, :], idt[:, 0, :])
                    k.copy(dst.p(g)[:, g * GH:(g + 1) * GH, :], bank[:64, :].m(J8), eng=("act" if tcnt % 2 else "dve"))
            st = {}
            for g in grp:
                NB, NK = nbq[g][n % 2], nkq[g][n % 2]
                hs = [g * GH + j for j in range(GH)]
                for j, h in enumerate(hs):
                    bank, col = pb[j // 4], (j % 4) * 128
                    k.mm(bank[:64, col:col + 128], fb[bi][:, h, :], a_[:, h, :, :].m(R_))
                for j, h in enumerate(hs):
                    bank, col = pb[2 + j // 4], (j % 4) * 128
                    k.mm(bank[:64, col:col + 128], fk[bi][:, h, :], a_[:, h, :, :].m(R_))
                for j, h in enumerate(hs):
                    k.mm(pb[4][:64, j * 64:(j + 1) * 64], a_[:, h, 0, :], fb[bi][:, h, :])
                for half in range(2):
                    k.tt(NB[:, half * 4:(half + 1) * 4, :], pb[half][:64, :].m(J4), mn[:, half * 4:(half + 1) * 4, :], ALU.mult)
                    k.tt(NK[:, half * 4:(half + 1) * 4, :], pb[2 + half][:64, :].m(J4), mn[:, half * 4:(half + 1) * 4, :], ALU.mult)
                Lc = Ls[g][0]
                k.tt(Lc[:], pb[4][:64, :].m(J8), ml[:], ALU.mult)
                Pc = Pm[g][0]
                k.tt(Pc[:], NB[:, :, 0:64], idt[:], ALU.add)
                st[g] = {"NB": NB, "NK": NK, "Lc": Lc, "Nc": None, "Pc": Pc, "hs": hs}
            return st

        def levels(n, st, hooks):
            for lvl in range(5):
                for g in grp:
                    S_ = st[g]
                    bL, bN, bP = LB[g]
                    Nv = (lambda j, NB=S_["NB"]: NB[:, j, 0:64]) if S_["Nc"] is None else (lambda j, Nc=S_["Nc"]: Nc[:, j, :])
                    for j in range(GH):
                        k.mm(bL[:64, j * 64:(j + 1) * 64], Nv(j), S_["Lc"][:, j, :])
                    if lvl < 4:
                        for j in range(GH):
                            k.mm(bN[:64, j * 64:(j + 1) * 64], S_["Lc"][:, j, :], Nv(j))
                if hooks is not None:
                    hooks[lvl]()
                for g in grp:
                    S_ = st[g]
                    bL, bN, bP = LB[g]
                    Ln = Ls[g][(lvl + 1) % 2]
                    k.copy(Ln[:], bL[:64, :].m(J8))
                    if lvl < 4:
                        Nn = Ns[g][lvl % 2]
                        k.copy(Nn[:], bN[:64, :].m(J8), eng="dve")
                        S_["Nc"] = Nn
                    S_["Lc"] = Ln
                for g in grp:
                    S_ = st[g]
                    bL, bN, bP = LB[g]
                    for j in range(GH):
                        k.mm(bP[:64, j * 64:(j + 1) * 64], S_["Lc"][:, j, :], S_["Pc"][:, j, :])
                for g in grp:
                    S_ = st[g]
                    bL, bN, bP = LB[g]
                    Pn = Pfin[g][n % 2] if lvl == 4 else Pm[g][(lvl + 1) % 2]
                    k.tt(Pn[:], bP[:64, :].m(J8), S_["Pc"][:], ALU.add)
                    S_["Pc"] = Pn

        def seq_steps(n, st):
            bi = n % NBUF
            a_ = ar[bi].p(0)
            ob = osb[n % 2]

            def sX():
                for g in grp:
                    S_ = st[g]; Hg = H.p(g); tvg = tv[bi].p(g)
                    for j, h in enumerate(S_["hs"]):
                        k.mm(BS[g][:64, j * 64:(j + 1) * 64], a_[:, h, 0, :], Hg[:, h, :], start=True, stop=False)
                        k.mm(BS[g][:64, j * 64:(j + 1) * 64], S_["NK"][:, j, 0:64], tvg[:, h, :], start=False, stop=True)
                for g in grp:
                    k.copy(xs[g][:], BS[g][:64, :].m(J8), eng=("act" if g == 0 else "dve"))

            def sU():
                for g in grp:
                    S_ = st[g]
                    for j, h in enumerate(S_["hs"]):
                        k.mm(BS[g][:64, j * 64:(j + 1) * 64], S_["Pc"][:, j, :], xs[g][:, j, :])
                for g in grp:
                    k.copy(us[g][:], BS[g][:64, :].m(J8), eng=("dve" if g == 0 else "act"))

            def sO():
                for g in grp:
                    S_ = st[g]; Hg = H.p(g); tvg = tv[bi].p(g)
                    for j, h in enumerate(S_["hs"]):
                        k.mm(BS[g][:64, j * 64:(j + 1) * 64], a_[:, h, 1, :], Hg[:, h, :], start=True, stop=False)
                        k.mm(BS[g][:64, j * 64:(j + 1) * 64], S_["NB"][:, j, 64:128], us[g][:, j, :], start=False, stop=False)
                        k.mm(BS[g][:64, j * 64:(j + 1) * 64], S_["NK"][:, j, 64:128], tvg[:, h, :], start=False, stop=True)
                for g in grp:
                    k.copy(ob.p(g)[:, g * GH:(g + 1) * GH, :], BS[g][:64, :].m(J8), eng=("act" if g == 0 else "dve"))
                for g in grp:
                    k.dma("pool", o_tok[n * 64:(n + 1) * 64, F0 + g * 512:F0 + (g + 1) * 512].m(
                        lambda a: a.rearrange("s (h v) -> s h v", v=64)), ob.p(g)[:, g * GH:(g + 1) * GH, :], ob)

            def sH():
                for g in grp:
                    S_ = st[g]; tbg, tkg, tvg = tb[bi].p(g), tk[bi].p(g), tv[bi].p(g)
                    for j, h in enumerate(S_["hs"]):
                        k.mm(BS[g][:64, j * 64:(j + 1) * 64], tbg[:, h, :], us[g][:, j, :], start=True, stop=False)
                        k.mm(BS[g][:64, j * 64:(j + 1) * 64], tkg[:, h, :], tvg[:, h, :], start=False, stop=True)

            def sF():
                for g in grp:
                    Hg = H.p(g)
                    k.tt(htmp[g][:], BS[g][:64, :].m(J8), Hg[:, g * GH:(g + 1) * GH, :], ALU.add)
                    k.tt(Hg[:, g * GH:(g + 1) * GH, :], htmp[g][:],
                         pcs[:, g * GH:(g + 1) * GH, n:n + 1].m(lambda a: a.broadcast_to([64, GH, 64])), ALU.mult)
            return [sX, sU, sO, sH, sF]

        prev = None
        for n in range(NC_):
            st = par_begin(n)
            levels(n, st, seq_steps(n - 1, prev) if prev is not None else None)
            prev = st
        for f in seq_steps(NC_ - 1, prev):
            f()
    k.end_stage()


def stage_post(k, T, mode, o_tok, szT, hT, bnT, vec, wo, C, hn, hn_off, is_output):
    k.begin_stage("C")
    TT = 256
    NT = T // TT
    rw = mode == "rwkv"
    fin = mode == "nsa_final"
    hres, hoff = hT
    vecs = k.sbuf("vecs", [128, 3, 16]); bds = k.sbuf("bds", [128, 128]); ones = k.sbuf("oness", [128, 128])
    idn = k.sbuf("idn", [128, 128])
    for sb, dr in [(vecs, vec), (bds, C["bdm"]), (ones, C["ones"]), (idn, C["IDN"])]:
        k.load(sb, dr)
    ot = k.sbuf("ot", [128, 2, D])
    o = k.sbuf("o", [128, 16, TT]); sz = k.sbuf("sz", [128, 16, TT]); h = k.sbuf("h", [128, 16, TT])
    bn = k.sbuf("bn", [128, 16, TT]) if rw else None
    ybuf = [k.sbuf(f"y{i}", [128, 16, TT]) for i in range(2)]
    hnew = k.sbuf("hnew", [128, 16, TT])
    wb = [k.sbuf(f"wb{i}", [128, 16, 128]) for i in range(3)]
    t1 = k.sbuf("t1", [128, TT]); t2 = k.sbuf("t2", [128, TT]); rstd = k.sbuf("rstd", [128, TT])
    pm = k.psum("pm", [128, 512]); pv = k.psum("pv", [128, 512])
    po = [k.psum(f"po{i}", [128, 512]) for i in range(2)]
    pn = k.psum("pn", [128, 512])
    pt = [k.psum(f"pt{i}", [128, 512]) for i in range(2)]
    wc = 0
    for t in range(NT):
        sl = slice(t * TT, (t + 1) * TT)
        y = ybuf[t % 2]
        for hh in range(2):
            k.dma("sp", ot.p(hh)[:, hh, :], o_tok[t * TT + hh * 128:t * TT + (hh + 1) * 128, :], ot)
        k.load(sz, szT[:, sl].m(FM))
        k.load(h, hres[:, hoff + t * TT:hoff + (t + 1) * TT].m(FM))
        if rw:
            k.load(bn, bnT[:, sl].m(FM))
        for c in range(NCH):
            ptc = pt[c % 2]
            for hh in range(2):
                k.tr(ptc[:, hh * 128:(hh + 1) * 128], ot.p(hh)[:, hh, c * 128:(c + 1) * 128], idn[:])
            k.copy(o.p(c)[:, c, :], ptc[:, :TT], eng=("act" if c % 2 else "dve"))
        for c in range(NCH):
            yc = y.p(c)[:, c, :]
            oc_ = o.p(c)[:, c, :]
            if rw:
                k.mm(pm[:, :TT], bds[:], oc_)
                k.tt(yc, oc_, pm[:, :TT], ALU.subtract)
                k.act(t1[:], yc, AF.Square)
                k.mm(pv[:, :TT], bds[:], t1[:])
                k.ts(t2[:], pv[:, :TT], 64e-5, ALU.add)
                k.act(t2[:], t2[:], AF.Ln)
                k.act(t2[:], t2[:], AF.Exp, scale=-0.5)
                k.tt(yc, yc, t2[:], ALU.mult)
                k.ts(yc, yc, vecs[:, 0, c:c + 1], ALU.mult, vecs[:, 1, c:c + 1], ALU.add)
                k.tt(yc, yc, bn[:, c, :], ALU.add)
                k.tt(yc, yc, sz[:, c, :], ALU.mult)
            else:
                k.tt(yc, oc_, sz[:, c, :], ALU.mult)
        for oc in range(NCH):
            w = wb[wc % 3]
            wc += 1
            k.load(w, wo[oc])
            ps = po[oc % 2]
            for c in range(NCH):
                k.mm(ps[:, :TT], w[:, c, :], y.p(c)[:, c, :], start=(c == 0), stop=(c == NCH - 1))
            k.tt(hnew.p(oc)[:, oc, :], ps[:, :TT], h[:, oc, :], ALU.add)
        if fin:
            for c in range(NCH):
                k.act(o.p(c)[:, c, :], hnew.p(c)[:, c, :], AF.Square)
            for c in range(NCH):
                k.mm(pn[:, :TT], ones[:], o.p(c)[:, c, :], start=(c == 0), stop=(c == NCH - 1))
            k.ts(rstd[:], pn[:, :TT], 1.0 / D, ALU.mult, 1e-6, ALU.add)
            k.act(rstd[:], rstd[:], AF.Ln)
            k.act(rstd[:], rstd[:], AF.Exp, scale=-0.5)
            for c in range(NCH):
                k.stt(o.p(c)[:, c, :], hnew.p(c)[:, c, :], vecs[:, 2, c:c + 1], rstd[:], ALU.mult, ALU.mult)
            for c in range(NCH):
                k.dma("pool", hn[c * 128:(c + 1) * 128, hn_off + t * TT:hn_off + (t + 1) * TT], o.p(c)[:, c, :], o,
                      is_output=is_output)
        else:
            for c in range(NCH):
                k.dma("pool", hn[c * 128:(c + 1) * 128, hn_off + t * TT:hn_off + (t + 1) * TT], hnew.p(c)[:, c, :], hnew,
                      is_output=is_output)
    k.end_stage()


def stage_proj(k, T, kinds, hT, gv, W, wg, gb, ct, st, C, Y, G):
    k.begin_stage("P")
    TT = 256
    NT = T // TT
    NOC = len(kinds)
    hres, hoff = hT
    with_gates = wg is not None
    gvs = k.sbuf("gvs", [128, 16]); wgs = k.sbuf("wgs", [128, 16, 48]); gbs = k.sbuf("gbs", [48, 1])
    cts = k.sbuf("cts", [128, T]); sts = k.sbuf("sts", [128, T]); rts = k.sbuf("rts", [128, 128])
    ones = k.sbuf("oness", [128, 128])
    lst = [(gvs, gv), (cts, ct), (sts, st), (rts, C["rt"]), (ones, C["ones"])]
    if with_gates:
        lst += [(wgs, wg), (gbs, gb)]
    for sb, dr in lst:
        k.load(sb, dr)
    h = k.sbuf("h", [128, 16, TT]); sq = k.sbuf("sq", [128, 16, TT])
    ubuf = [k.sbuf(f"u{i}", [128, 16, TT]) for i in range(2)]
    rstd = k.sbuf("rstd", [128, TT])
    wb = [k.sbuf(f"wb{i}", [128, 16, 128]) for i in range(3)]
    ys = [k.sbuf(f"ys{i}", [128, TT]) for i in range(2)]
    qs = k.sbuf("qs", [128, TT]); t1 = k.sbuf("t1", [128, TT]); gs = k.sbuf("gsb", [48, TT])
    pn = k.psum("pn", [128, 512]); pr = k.psum("pr", [128, 512]); pg = k.psum("pg", [128, 512])
    po = [k.psum(f"po{i}", [128, 512]) for i in range(2)]
    wc = 0
    for t in range(NT):
        sl = slice(t * TT, (t + 1) * TT)
        u = ubuf[t % 2]
        k.load(h, hres[:, hoff + t * TT:hoff + (t + 1) * TT].m(FM))
        rmsnorm_fm(k, h, u, gvs[:, :], ones, pn, sq, rstd, TT)
        if with_gates:
            for c in range(NCH):
                k.mm(pg[:48, :TT], wgs[:, c, :], u.p(c)[:, c, :], start=(c == 0), stop=(c == NCH - 1))
            k.act(gs[:], pg[:48, :TT], AF.Sigmoid, bias=gbs[:, 0:1])
            k.dma("pool", G[:, sl], gs[:], gs)
        for oc in range(NOC):
            w = wb[wc % 3]
            k.load(w, W[oc])
            ps = po[wc % 2]
            y = ys[wc % 2]
            wc += 1
            for c in range(NCH):
                k.mm(ps[:, :TT], w[:, c, :], u.p(c)[:, c, :], start=(c == 0), stop=(c == NCH - 1))
            kind = kinds[oc]
            if kind == "copy":
                k.copy(y[:], ps[:, :TT])
            elif kind == "silu":
                k.act(y[:], ps[:, :TT], AF.Silu)
            else:
                k.copy(qs[:], ps[:, :TT])
                k.mm(pr[:, :TT], rts[:], qs[:])
                k.tt(t1[:], qs[:], cts[:, sl], ALU.mult)
                k.tt(y[:], pr[:, :TT], sts[:, sl], ALU.mult)
                k.tt(y[:], y[:], t1[:], ALU.add)
            k.dma("pool", Y[oc * 128:(oc + 1) * 128, sl], y[:], y)
    k.end_stage()


def stage_cmp(k, T, KV, PE, W1, W2, ct, st, C, KCo, VCo):
    k.begin_stage("M")
    NCP = T // 16
    NCM = NCP - 1
    CW = min(128, NCP)
    NCC = NCP // CW
    pes = k.sbuf("pes", [128, 2, 32]); w1s = k.sbuf("w1s", [128, 2, 32, 256]); w2s = k.sbuf("w2s", [128, 2, 2, 128])
    cts = k.sbuf("cts", [128, NCP]); sts = k.sbuf("sts", [128, NCP]); rts = k.sbuf("rts", [128, 128])
    idn = k.sbuf("idn", [128, 128])
    for j in range(2):
        k.dma("sp", pes.p(j)[:, j, :], PE[j], pes)
        k.dma("sp", w1s.p(j)[:, j, :, :], W1[j], w1s)
        k.dma("sp", w2s.p(j)[:, j, :, :], W2[j], w2s)
    for sb, dr in [(cts, ct), (sts, st), (rts, C["rt"]), (idn, C["IDN"])]:
        k.load(sb, dr)
    xc = [k.sbuf(f"xc{i}", [128, T]) for i in range(2)]
    xl = [k.sbuf(f"xl{i}", [128, NCP]) for i in range(3)]
    hid = k.sbuf("hid", [128, 2, NCP]); ob = [k.sbuf(f"ob{i}", [128, NCP]) for i in range(2)]
    vt = [k.sbuf(f"vt{i}", [CW, NCC, 128]) for i in range(2)]
    qs = k.sbuf("qs", [128, NCP]); t1 = k.sbuf("t1", [128, NCP])
    p1 = [k.psum(f"p1{i}", [128, 512]) for i in range(2)]
    p2 = k.psum("p2", [128, 512]); pr = k.psum("pr", [128, 512])
    it = 0
    for g in range(4):
        for j in range(2):
            x = xc[it % 2]
            o = ob[it % 2]
            it += 1
            k.load(x, KV[(j * 4 + g) * 128:(j * 4 + g + 1) * 128, :])
            for l in range(32):
                xb = xl[l % 3]
                src = x[:, l:l + 16 * (NCM - 1) + 1:16]
                k.ts(xb[:, :NCM], src, pes.p(j)[:, j, l:l + 1], ALU.add)
                for hc in range(2):
                    k.mm(p1[hc][:, :NCM], w1s.p(j)[:, j, l, hc * 128:(hc + 1) * 128], xb[:, :NCM],
                         start=(l == 0), stop=(l == 31))
            for hc in range(2):
                k.act(hid[:, hc, :NCM], p1[hc][:, :NCM], AF.Silu)
            for hc in range(2):
                k.mm(p2[:, :NCM], w2s.p(j)[:, j, hc, :], hid[:, hc, :NCM], start=(hc == 0), stop=(hc == 1))
            k.memset(o[:], 0.0)
            if j == 0:
                k.copy(qs[:, :NCM], p2[:, :NCM])
                k.mm(pr[:, :NCM], rts[:], qs[:, :NCM])
                k.tt(t1[:, :NCM], qs[:, :NCM], cts[:, :NCM], ALU.mult)
                k.tt(o[:, :NCM], pr[:, :NCM], sts[:, :NCM], ALU.mult)
                k.tt(o[:, :NCM], o[:, :NCM], t1[:, :NCM], ALU.add)
                k.dma("pool", KCo[g], o[:], o)
            else:
                k.copy(o[:, :NCM], p2[:, :NCM])
                v_ = vt[g % 2]
                for cc in range(NCC):
                    k.tr(pr[:CW, cc * 128:(cc + 1) * 128], o[:, cc * CW:(cc + 1) * CW], idn[:])
                for cc in range(NCC):
                    k.copy(v_[:, cc, :], pr[:CW, cc * 128:(cc + 1) * 128])
                k.dma("pool", VCo[g], v_[:], v_)
    k.end_stage()


def stage_attn(k, T, Y, G, KV, KCo, VCo, C, o_tok):
    k.begin_stage("T")
    nc = k.nc
    NQ = T // 128
    NBLK = T // 64
    NCP = T // 16
    CW = min(128, NCP)
    NCC = NCP // CW
    ovs = k.sbuf("ovs", [CW, NCC, NBLK]); band = k.sbuf("band", [128, 16]); rv = k.sbuf("rv", [128, 1])
    keep = k.sbuf("keep", [128, NQ, NBLK]); addc = k.sbuf("addc", [128, NQ, NBLK])
    caus = k.sbuf("caus", [128, 128]); wlo = k.sbuf("wlo", [128, 128]); idn = k.sbuf("idn", [128, 128])
    for sb, dr in [(ovs, C["OV"]), (band, C["BAND"]), (rv, C["RV"]), (keep, C["KEEP"]), (addc, C["ADDC"]),
                   (caus, C["CAUS"]), (wlo, C["WLO"]), (idn, C["IDN"])]:
        k.load(sb, dr)
    kcs = k.sbuf("kcs", [128, NCP]); vcs = k.sbuf("vcs", [CW, NCC, 128])
    kss = k.sbuf("kss", [128, T]); vss = k.sbuf("vss", [128, NQ, 128])
    kws = k.sbuf("kws", [128, T]); vws = k.sbuf("vws", [128, NQ, 128])
    gts = k.sbuf("gts", [128, NQ, 12])
    qb = [k.sbuf(f"qb{i}", [128, 4, 128]) for i in range(2)]
    accb = [k.sbuf(f"acc{i}", [128, 4, 128]) for i in range(2)]
    srow = [k.sbuf(f"srow{i}", [128, T]) for i in range(2)]
    etb = [k.sbuf(f"et{i}", [128, NQ, 128]) for i in range(2)]
    scb = [k.sbuf(f"sc{i}", [128, NCP]) for i in range(2)]
    pcb = [k.sbuf(f"pc{i}", [128, NCP]) for i in range(2)]
    pcTb = [k.sbuf(f"pcT{i}", [CW, NCC, 128]) for i in range(2)]
    imp = k.sbuf("imp", [128, NBLK]); imp2 = k.sbuf("imp2", [128, NBLK]); seln = k.sbuf("seln", [128, NBLK])
    m8 = k.sbuf("m8", [128, 8]); m8b = k.sbuf("m8b", [128, 8])
    st_ = {n: [k.sbuf(f"st_{n}{i}", [128, 1]) for i in range(4)] for n in ["mx", "nmx", "sm", "ri"]}
    pS = [k.psum(f"pS{i}", [128, 512]) for i in range(2)]
    pT = [k.psum(f"pT{i}", [128, 512]) for i in range(2)]
    pO = [k.psum(f"pO{i}", [128, 512]) for i in range(2)]
    pI = k.psum("pI", [128, 512]); pC = k.psum("pC", [128, 512])
    ctr = {"s": 0, "t": 0, "o": 0, "st": 0, "row": 0, "et": 0}
    JC = lambda a: a.rearrange("p (j c) -> p j c", c=128)

    def nxt(name, n=2):
        v = ctr[name]
        ctr[name] = v + 1
        return v % n

    def softmax_pv(S, ntile, vsrc, kt0, gcol, acc_v):
        si = nxt("st", 4)
        mx, nmx, sm, ri = (st_[n][si] for n in ["mx", "nmx", "sm", "ri"])
        k.op("dve", nc.vector.tensor_reduce, [S], [mx[:]], out=mx[:].ap, in_=S.ap, axis=AX.X, op=ALU.max)
        k.ts(nmx[:], mx[:], -1.0, ALU.mult)
        k.act(S, S, AF.Exp, bias=nmx[:], accum_out=sm[:])
        et = etb[nxt("et")]
        for g0 in range(0, ntile, 4):
            n = min(4, ntile - g0)
            pt = pT[nxt("t")]
            for j in range(n):
                k.tr(pt[:, j * 128:(j + 1) * 128], S[:, (g0 + j) * 128:(g0 + j + 1) * 128], idn[:])
            k.copy(et[:, g0:g0 + n, :], pt[:, :n * 128].m(JC), eng=("act" if (g0 // 4) % 2 == 0 else "dve"))
        po = pO[nxt("o")]
        for j in range(ntile):
            k.mm(po[:, :128], et[:, j, :], vsrc[:, kt0 + j, :], start=(j == 0), stop=(j == ntile - 1))
        k.op("dve", nc.vector.reciprocal, [sm[:]], [ri[:]], out=ri[:].ap, in_=sm[:].ap)
        k.tt(ri[:], ri[:], gcol, ALU.mult)
        k.stt(acc_v, po[:, :128], ri[:], acc_v, ALU.mult, ALU.add)

    def tile_jobs(g, i):
        qt = qb[i % 2]
        acc = accb[i % 2]
        sel = i >= 8
        NCi = min(8 * i + 8, NCP)
        ncc = (NCi + CW - 1) // CW
        nk = i + 1
        kt0 = max(0, i - 4)
        nw = i - kt0 + 1
        jobs = []

        def cmp_A(r):
            def f():
                if r == 0:
                    k.dma("sp", qt[:], Y[g * 512:(g + 1) * 512, i * 128:(i + 1) * 128].m(
                        lambda a: a.rearrange("(r d) t -> d r t", d=128)), qt)
                sc = scb[r % 2]
                k.mm(pC[:, :NCi], qt[:, r, :], kcs[:, :NCi])
                lo = max(NCi - 16, 0)
                bw = NCi - lo
                if lo > 0:
                    k.copy(sc[:, :lo], pC[:, :lo])
                k.tt(sc[:, lo:NCi], pC[:, lo:NCi], band[:, 16 - bw:16], ALU.add)
            return f

        def cmp_B(r):
            def f():
                sc = scb[r % 2]
                pc = pcb[r % 2]
                pcT = pcTb[r % 2]
                si = nxt("st", 4)
                mx, nmx, sm, ri = (st_[n][si] for n in ["mx", "nmx", "sm", "ri"])
                k.op("dve", nc.vector.tensor_reduce, [sc[:, :NCi]], [mx[:]], out=mx[:].ap, in_=sc[:, :NCi].ap,
                     axis=AX.X, op=ALU.max)
                k.ts(nmx[:], mx[:], -1.0, ALU.mult)
                k.act(sc[:, :NCi], sc[:, :NCi], AF.Exp, bias=nmx[:], accum_out=sm[:])
                k.op("dve", nc.vector.reciprocal, [sm[:]], [ri[:]], out=ri[:].ap, in_=sm[:].ap)
                if i == 0:
                    k.tt(ri[:], ri[:], rv[:], ALU.mult)
                k.ts(pc[:, :NCi], sc[:, :NCi], ri[:], ALU.mult)
                pt = pT[nxt("t")]
                for cc in range(ncc):
                    w = min(CW, NCi - cc * CW)
                    k.tr(pt[:w, cc * 128:(cc + 1) * 128], pc[:, cc * CW:cc * CW + w], idn[:])
                for cc in range(ncc):
                    w = min(CW, NCi - cc * CW)
                    k.copy(pcT[:w, cc, :], pt[:w, cc * 128:(cc + 1) * 128])
                po = pO[nxt("o")]
                for cc in range(ncc):
                    w = min(CW, NCi - cc * CW)
                    k.mm(po[:, :128], pcT[:w, cc, :], vcs[:w, cc, :], start=(cc == 0), stop=(cc == ncc - 1))
                if sel:
                    for cc in range(ncc):
                        w = min(CW, NCi - cc * CW)
                        k.mm(pI[:, :NBLK], pcT[:w, cc, :], ovs[:w, cc, :], start=(r == 0 and cc == 0),
                             stop=(r == 3 and cc == ncc - 1))
                k.ts(acc.p(r)[:, r, :], po[:, :128], gts[:, i, r * 3:r * 3 + 1], ALU.mult)
            return f

        def sel_A(r):
            def f():
                if r == 0 and sel:
                    k.tt(imp[:], pI[:, :NBLK], keep[:, i, :], ALU.mult)
                    k.tt(imp[:], imp[:], addc[:, i, :], ALU.add)
                    k.op("dve", nc.vector.max, [imp[:]], [m8[:]], out=m8[:].ap, in_=imp[:].ap)
                    k.op("dve", nc.vector.match_replace, [m8[:], imp[:]], [imp2[:]], out=imp2[:].ap,
                         in_to_replace=m8[:].ap, in_values=imp[:].ap, imm_value=-3.0e38)
                    k.op("dve", nc.vector.max, [imp2[:]], [m8b[:]], out=m8b[:].ap, in_=imp2[:].ap)
                    k.ts(seln[:], imp[:], m8b[:, 7:8], ALU.is_ge)
                    k.ts(seln[:], seln[:], -1.0, ALU.add, 1.0e30, ALU.mult)
                S = srow[(2 * i + r) % 2]
                for kb in range(0, nk, 4):
                    ke = min(nk, kb + 4)
                    ps = pS[nxt("s")]
                    k.mm(ps[:, :(ke - kb) * 128], qt[:, r, :], kss[:, kb * 128:ke * 128])
                    has_diag = ke == nk
                    nfull = (ke - kb) - (1 if has_diag else 0)
                    if nfull > 0:
                        if sel:
                            k.tt(S[:, kb * 128:(kb + nfull) * 128].m(lambda a: a.rearrange("p (j c) -> p j c", c=64)),
                                 ps[:, :nfull * 128].m(lambda a: a.rearrange("p (j c) -> p j c", c=64)),
                                 seln[:, 2 * kb:2 * (kb + nfull)].m(lambda a, nf=nfull: a.unsqueeze(2).broadcast_to([128, 2 * nf, 64])),
                                 ALU.add)
                        else:
                            k.copy(S[:, kb * 128:(kb + nfull) * 128], ps[:, :nfull * 128])
                    if has_diag:
                        k.tt(S[:, i * 128:(i + 1) * 128], ps[:, nfull * 128:(nfull + 1) * 128], caus[:], ALU.add)
            return f

        def sel_B(r):
            def f():
                S = srow[(2 * i + r) % 2]
                softmax_pv(S[:, :nk * 128], nk, vss, 0, gts[:, i, r * 3 + 1:r * 3 + 2], acc.p(r)[:, r, :])
            return f

        def win_A(r):
            def f():
                S = srow[(2 * i + r) % 2]
                for g0 in range(0, nw, 4):
                    ge = min(nw, g0 + 4)
                    ps = pS[nxt("s")]
                    k.mm(ps[:, :(ge - g0) * 128], qt[:, r, :], kws[:, (kt0 + g0) * 128:(kt0 + ge) * 128])
                    for j in range(g0, ge):
                        kt = kt0 + j
                        dst = S[:, j * 128:(j + 1) * 128]
                        src = ps[:, (j - g0) * 128:(j - g0 + 1) * 128]
                        if kt == i:
                            k.tt(dst, src, caus[:], ALU.add)
                        elif kt == i - 4:
                            k.tt(dst, src, wlo[:], ALU.add)
                        else:
                            k.copy(dst, src)
            return f

        def win_B(r):
            def f():
                S = srow[(2 * i + r) % 2]
                softmax_pv(S[:, :nw * 128], nw, vws, kt0, gts[:, i, r * 3 + 2:r * 3 + 3], acc.p(r)[:, r, :])
                k.dma("pool", o_tok[i * 128:(i + 1) * 128, g * 512 + r * 128:g * 512 + (r + 1) * 128], acc.p(r)[:, r, :], acc)
            return f

        for r in range(4):
            jobs.append((False, cmp_A(r), cmp_B(r)))
        for r in range(4):
            jobs.append((r == 0 and sel, sel_A(r), sel_B(r)))
        for r in range(4):
            jobs.append((False, win_A(r), win_B(r)))
        return jobs

    for g in range(4):
        k.load(kcs, KCo[g]); k.load(vcs, VCo[g])
        k.load(kss, KV[(8 + g) * 128:(9 + g) * 128, :])
        k.load(kws, KV[(16 + g) * 128:(17 + g) * 128, :])
        for (vidx, vdst) in [(12 + g, vss), (20 + g, vws)]:
            k.load(srow[0], KV[vidx * 128:(vidx + 1) * 128, :])
            for i0 in range(0, NQ, 4):
                n = min(4, NQ - i0)
                pt = pT[nxt("t")]
                for j in range(n):
                    k.tr(pt[:, j * 128:(j + 1) * 128], srow[0][:, (i0 + j) * 128:(i0 + j + 1) * 128], idn[:])
                k.copy(vdst[:, i0:i0 + n, :], pt[:, :n * 128].m(JC), eng=("act" if (i0 // 4) % 2 == 0 else "dve"))
        k.dma("sp", srow[1][:12, :], G[g * 12:(g + 1) * 12, :], srow[1])
        for i in range(NQ):
            k.tr(pI[:, i * 12:(i + 1) * 12], srow[1][:12, i * 128:(i + 1) * 128], idn[:12, :12])
        k.copy(gts[:], pI[:, :NQ * 12].m(lambda a: a.rearrange("p (i c) -> p i c", c=12)))
        jobs = []
        for i in range(NQ):
            jobs += tile_jobs(g, i)
        pending = None
        for (barrier, fa, fb_) in jobs:
            if barrier and pending is not None:
                pending()
                pending = None
            fa()
            if pending is not None:
                pending()
            pending = fb_
        if pending is not None:
            pending()
    k.end_stage()


DEBUG = False


def build_fused(T):
    k = K()
    NC_ = T // 64
    NQ, NBLK, NCP = T // 128, T // 64, T // 16
    CW = min(128, NCP)
    NCC = NCP // CW
    ins = {}

    def ein(name, shape):
        ins[name] = k.dram(name, shape)
        return ins[name]
    xT = ein("xT", [D, T + 1])
    WA, WC = [], []
    for i in range(2):
        w = {"vec": ein(f"A{i}_vec", [128, len(VEC_A), 16]), "wq": ein(f"A{i}_wq", [4, 16, 128, 16, 128]),
             "w1": ein(f"A{i}_w1", [128, 16, 96]), "w2": ein(f"A{i}_w2", [96, D]),
             "a1": ein(f"A{i}_a1", [128, 16, 96]), "a2": ein(f"A{i}_a2", [96, D])}
        if i > 0:
            w["v1"] = ein(f"A{i}_v1", [128, 16, 64]); w["v2"] = ein(f"A{i}_v2", [64, D])
        WA.append(w)
        WC.append({"vec": ein(f"C{i}_vec", [128, 3, 16]), "wo": ein(f"C{i}_wo", [16, 128, 16, 128])})
    C = {}
    for name, shape in [("bd", [128, 128]), ("bdm", [128, 128]), ("ones", [128, 128]), ("rmask", [128, 128]),
                        ("IDN", [128, 128]), ("MN", [64, 8, 128]), ("ML", [64, 8, 64]), ("ID", [64, 8, 64]),
                        ("rt", [128, 128]), ("OV", [CW, NCC, NBLK]), ("BAND", [128, 16]), ("RV", [128, 1]),
                        ("KEEP", [128, NQ, NBLK]), ("ADDC", [128, NQ, NBLK]), ("CAUS", [128, 128]), ("WLO", [128, 128])]:
        C[name] = ein(name, shape)
    ct_kv, st_kv = ein("ct_kv", [128, T]), ein("st_kv", [128, T])
    ct_q, st_q = ein("ct_q", [128, T]), ein("st_q", [128, T])
    ct_c, st_c = ein("ct_c", [128, NCP]), ein("st_c", [128, NCP])
    kv_gv, kv_W = ein("kv_gv", [128, 16]), ein("kv_W", [24, 128, 16, 128])
    PE, W1, W2 = ein("PE", [2, 128, 32]), ein("W1", [2, 128, 32, 256]), ein("W2", [2, 128, 2, 128])
    WN = []
    for j in range(2):
        WN.append({"gv": ein(f"N{j}_gv", [128, 16]), "W": ein(f"N{j}_W", [32, 128, 16, 128]),
                   "wg": ein(f"N{j}_wg", [128, 16, 48]), "gb": ein(f"N{j}_gb", [48, 1]),
                   "vec": ein(f"N{j}_vec", [128, 3, 16]), "wo": ein(f"N{j}_wo", [16, 128, 16, 128])})
    outT = k.dram("outT", [D, T], kind="ExternalOutput")
    itn = lambda name, shape: k.dram(name, shape, kind=("ExternalOutput" if DEBUG else "Internal"))
    hbufs = [k.dram(f"hbuf{i}", [D, T + 1], kind=("ExternalOutput" if DEBUG else "Internal")) for i in range(3)]
    S = {n: itn("s_" + n, [D, T]) for n in A_OUTS}
    vf0 = itn("vf0", [D, T])
    pc = itn("pc", [D, NC_])
    o_tok = itn("o_tok", [T, D])
    KV = itn("KV", [3072, T]); Y = itn("Y", [4096, T]); G = itn("G", [48, T])
    KCo = itn("KCo", [4, 128, NCP]); VCo = itn("VCo", [4, CW, NCC, 128])

    k.begin_stage("Z")
    zt = k.sbuf("zt", [128, 16, 1])
    k.memset(zt[:], 0.0)
    for hb in hbufs:
        k.dma("sp", hb[:, 0:1].m(FM), zt[:], zt, nonc=True)
    k.end_stage()

    kinds_kv = ["rope" if (c // 4) in (2, 4) else "copy" for c in range(24)]
    kinds_in = ["rope"] * 16 + ["silu"] * 16
    hseq = [xT] + hbufs
    hidx = 0
    hin, hout = hseq[0], hseq[1]
    for i in range(2):
        outs = dict(S)
        if i == 0:
            outs["vv"] = vf0
        stage_rwkv_pre(k, T, i > 0, hin, WA[i], C, outs, pc, vf0)
        stage_scan(k, T, outs, pc, C, o_tok)
        if DEBUG == "B0":
            return k.finish()
        stage_post(k, T, "rwkv", o_tok, S["sz"], (hin, 1), S["bonus"], WC[i]["vec"], WC[i]["wo"], C, hout, 1, False)
        hidx += 1
        hin, hout = hseq[hidx], hseq[min(hidx + 1, 3)]
    stage_proj(k, T, kinds_kv, (hin, 1), kv_gv, kv_W, None, None, ct_kv, st_kv, C, KV, G)
    stage_cmp(k, T, KV, PE, W1, W2, ct_c, st_c, C, KCo, VCo)
    for j in range(2):
        stage_proj(k, T, kinds_in, (hin, 1), WN[j]["gv"], WN[j]["W"], WN[j]["wg"], WN[j]["gb"], ct_q, st_q, C, Y, G)
        stage_attn(k, T, Y, G, KV, KCo, VCo, C, o_tok)
        fin = j == 1
        stage_post(k, T, "nsa_final" if fin else "nsa", o_tok, Y[2048:4096, :], (hin, 1), None, WN[j]["vec"], WN[j]["wo"], C,
                   outT if fin else hout, 0 if fin else 1, fin)
        hidx += 1
        hin, hout = hseq[min(hidx, 3)], hseq[min(hidx + 1, 3)]
    return k.finish()


def vec16(v):
    return np.ascontiguousarray(np.asarray(v, np.float32).reshape(16, 128).T)


def wlay(W):
    n_out = W.shape[1]
    oc = n_out // 128
    return np.ascontiguousarray(W.reshape(16, 128, oc, 128).transpose(2, 1, 0, 3))


def w1lay(W):
    return np.ascontiguousarray(W.reshape(16, 128, W.shape[1]).transpose(1, 0, 2))


def const_bd():
    m = np.zeros((128, 128), np.float32)
    m[:64, :64] = 1.0
    m[64:, 64:] = 1.0
    return m


def scan_consts():
    s = np.arange(64)[:, None]
    t = np.arange(64)[None, :]
    su = (s < t).astype(np.float32)
    ui = (s <= t).astype(np.float32)
    mn = np.tile(np.concatenate([su, ui], axis=1)[:, None, :], (1, 8, 1))
    ml = np.tile((t < s).astype(np.float32)[:, None, :], (1, 8, 1))
    idt = np.tile(np.eye(64, dtype=np.float32)[:, None, :], (1, 8, 1))
    return np.ascontiguousarray(mn), np.ascontiguousarray(ml), np.ascontiguousarray(idt)


def rope_consts(pos, scale=1.0):
    half = 16
    inv = (500000.0 ** (-np.arange(half, dtype=np.float32) / half)).astype(np.float32)
    ang = pos.astype(np.float32)[None, :] * inv[:, None]
    ct = np.ones((128, len(pos)), np.float32)
    st = np.zeros((128, len(pos)), np.float32)
    ct[:16] = np.cos(ang); ct[16:32] = np.cos(ang)
    st[:16] = np.sin(ang); st[16:32] = np.sin(ang)
    rt = np.zeros((128, 128), np.float32)
    for i in range(16):
        rt[i + 16, i] = -1.0
        rt[i, i + 16] = 1.0
    return (ct * scale).astype(np.float32), (st * scale).astype(np.float32), rt


def attn_consts(T):
    NQ, NBLK, NCP = T // 128, T // 64, T // 16
    NCM = NCP - 1
    CW = min(128, NCP)
    NCC = NCP // CW
    ov = np.zeros((NCP, NBLK), np.float32)
    for c in range(NCM):
        for l in range(32):
            ov[c, (16 * c + l) // 64] += 1.0 / 32.0
    OV = np.ascontiguousarray(ov.reshape(NCC, CW, NBLK).transpose(1, 0, 2))
    t = np.arange(128)[:, None]
    j = np.arange(16)[None, :]
    band = np.where(t >= 16 * j - 97, 0.0, NEG).astype(np.float32)
    rv = (np.arange(128) >= 31).astype(np.float32).reshape(128, 1)
    tq = (np.arange(NQ)[None, :] * 128 + np.arange(128)[:, None])[:, :, None]
    cur = tq // 64
    blk = np.arange(NBLK)[None, None, :]
    forced = (blk == 0) | (blk == cur) | (blk == cur - 1)
    future = blk > cur
    keep = np.where(forced | future, 0.0, 1.0).astype(np.float32)
    addc = np.where(forced, 1.0e4, np.where(future, NEG, 0.0)).astype(np.float32)
    r = np.arange(128)[:, None]
    c = np.arange(128)[None, :]
    caus = np.where(c <= r, 0.0, NEG).astype(np.float32)
    wlo = np.where(c > r, 0.0, NEG).astype(np.float32)
    return {"OV": OV, "BAND": band, "RV": rv, "KEEP": np.ascontiguousarray(keep), "ADDC": np.ascontiguousarray(addc),
            "CAUS": caus, "WLO": wlo, "IDN": np.eye(128, dtype=np.float32)}


_PROGS = {}


def forward(x, P):
    B, T, _ = x.shape
    if T not in _PROGS:
        _PROGS[T] = build_fused(T)
    nc = _PROGS[T]
    NCP = T // 16
    z = np.zeros(D, np.float32)
    com = {}
    for i in range(2):
        vres = i > 0
        vecs = {"ng": P["a_norm_g"][i], "w0": P["a_w0"][i], "a0": P["a_a0"][i],
                "v0": P["a_v0"][i - 1] if vres else z,
                "k_k": P["a_k_k"][i], "k_a": P["a_k_a"][i], "r_k": P["a_r_k"][i].reshape(D)}
        for j in range(6):
            vecs[f"mu{j}"] = P["a_mu"][i, j]
        com[f"A{i}_vec"] = np.ascontiguousarray(np.stack([vec16(vecs[n]) for n in VEC_A], axis=1))
        com[f"A{i}_wq"] = np.stack([wlay(P["a_w_rkvz"][i, j]) for j in range(4)])
        com[f"A{i}_w1"] = w1lay(P["a_w1"][i]); com[f"A{i}_w2"] = np.ascontiguousarray(P["a_w2"][i])
        com[f"A{i}_a1"] = w1lay(P["a_a1"][i]); com[f"A{i}_a2"] = np.ascontiguousarray(P["a_a2"][i])
        if vres:
            com[f"A{i}_v1"] = w1lay(P["a_v1"][i - 1]); com[f"A{i}_v2"] = np.ascontiguousarray(P["a_v2"][i - 1])
        com[f"C{i}_vec"] = np.ascontiguousarray(np.stack([vec16(v) for v in (P["a_ln_g"][i], P["a_ln_b"][i], z)], axis=1))
        com[f"C{i}_wo"] = wlay(P["a_w_o"][i])
    rmask = np.ones((128, 128), np.float32)
    rmask[:, 0::64] = 0.0
    mn, ml, idt = scan_consts()
    com.update({"bd": const_bd(), "bdm": (const_bd() / 64.0).astype(np.float32), "ones": np.ones((128, 128), np.float32),
                "rmask": rmask, "MN": mn, "ML": ml, "ID": idt})
    com.update(attn_consts(T))
    pos = np.arange(T)
    com["ct_kv"], com["st_kv"], com["rt"] = rope_consts(pos)
    com["ct_q"], com["st_q"], _ = rope_consts(pos, 128.0 ** -0.5)
    com["ct_c"], com["st_c"], _ = rope_consts(np.arange(NCP) * 16 + 31)
    com["kv_gv"] = vec16(P["kv_norm_g"]); com["kv_W"] = wlay(P["kv_w"])
    com["PE"] = np.ascontiguousarray(P["cmp_pe"].transpose(0, 2, 1))
    com["W1"] = np.ascontiguousarray(P["cmp_w1"].reshape(2, 32, 128, 256).transpose(0, 2, 1, 3))
    com["W2"] = np.ascontiguousarray(P["cmp_w2"].reshape(2, 2, 128, 128).transpose(0, 2, 1, 3))
    for j in range(2):
        w_in = P["b_w_in"][j]
        com[f"N{j}_gv"] = vec16(P["b_norm_g"][j])
        com[f"N{j}_W"] = wlay(np.ascontiguousarray(np.concatenate([w_in[:, :2048], w_in[:, 2096:]], axis=1)))
        com[f"N{j}_wg"] = w1lay(np.ascontiguousarray(w_in[:, 2048:2096]))
        com[f"N{j}_gb"] = np.ascontiguousarray(P["b_gate_b"][j].reshape(48, 1))
        com[f"N{j}_vec"] = np.ascontiguousarray(np.stack([vec16(v) for v in (z, z, P["final_g"])], axis=1))
        com[f"N{j}_wo"] = wlay(P["b_w_o"][j])
    in_maps = []
    for c in range(NCORES):
        b = c % B
        m = dict(com)
        m["xT"] = np.ascontiguousarray(np.concatenate([np.zeros((1, D), np.float32), x[b]], axis=0).T)
        in_maps.append(m)
    res = run_bass_kernel_spmd(nc, in_maps, core_ids=list(range(NCORES)))
    if DEBUG:
        forward.dbg = [[res.results[b][f"hbuf{i}"][:, 1:].T for b in range(B)] for i in range(3)]
    out = np.stack([res.results[b]["outT"].T for b in range(B)])
    return np.ascontiguousarray(out)


def kernel(**inputs):
    P = {k_: np.asarray(v, dtype=np.float32) for k_, v in inputs.items()}
    return forward(P["x"], P).astype(np.float32)
```
